# Optimizing a Trainium2 kernel written in Bass

```python
import jax, jax.numpy as jnp
from jax import lax
import numpy as np

D_MODEL = 1024
BATCH = 16
SEQ = 256
DEPTH = 1
DEC_BATCH = 2
DEC_SEQ = 4096
PAST_LEN = 512

GRID_W = 64
HEAD_DIM = 64
D_A = D_MODEL
N_HEADS_A = D_A // HEAD_DIM
D_B = D_MODEL // 2
POOL_WINDOWS = (2, 4, 8, 16)
POOL_GROUPS = 4
POOL_GROUP_DIM = D_B // POOL_GROUPS
LORA_W = 64
LORA_A = 64
LORA_G = 128
D_FF = 2816
N_MOD = 9
SHIFT_W = 3 * D_A + 2 * LORA_W + 2 * LORA_A + LORA_G
MIX_IN = SHIFT_W + D_B + 2 * D_MODEL
ALPHA = (2 * DEPTH) ** 0.25
BETA = (8 * DEPTH) ** -0.25
LN_EPS = 1e-5
GN_EPS = 64e-5

kernel_name = 'hybrid_rwkv7_pool_diffusion_step'


def _layer_norm(x, g, b):
    xf = x.astype(jnp.float32)
    mu = jnp.mean(xf, -1, keepdims=True)
    var = jnp.mean(jnp.square(xf - mu), -1, keepdims=True)
    return ((xf - mu) * lax.rsqrt(var + LN_EPS) * g + b).astype(x.dtype)


def _swiglu(h, w_in, w_out):
    gate, up = jnp.split(h @ w_in, 2, axis=-1)
    return (jax.nn.silu(gate) * up) @ w_out


def _modulation(cvec, w_mod, b_mod):
    m = jax.nn.silu(cvec) @ w_mod + b_mod
    return m.reshape(cvec.shape[0], 1, N_MOD, D_MODEL)


def _shift_seq(p):
    b, t, c = p.shape
    q = p.reshape(b, t, c // 2, 2)
    prev = jnp.pad(q[:, :-1, :, 0], ((0, 0), (1, 0), (0, 0)))
    nxt = jnp.pad(q[:, 1:, :, 1], ((0, 0), (0, 1), (0, 0)))
    return jnp.stack([prev, nxt], axis=-1).reshape(b, t, c)


def _shift_grid(p):
    b, t, c = p.shape
    rows = t // GRID_W
    q = p.reshape(b, rows, GRID_W, c // 4, 4)
    left = jnp.pad(q[:, :, :-1, :, 0], ((0, 0), (0, 0), (1, 0), (0, 0)))
    right = jnp.pad(q[:, :, 1:, :, 1], ((0, 0), (0, 0), (0, 1), (0, 0)))
    up = jnp.pad(q[:, :-1, :, :, 2], ((0, 0), (1, 0), (0, 0), (0, 0)))
    down = jnp.pad(q[:, 1:, :, :, 3], ((0, 0), (0, 1), (0, 0), (0, 0)))
    return jnp.stack([left, right, up, down], axis=-1).reshape(b, t, c)


def _centred_pool_residual(p):
    b, t, c = p.shape
    pf = p.astype(jnp.float32)
    pos = jnp.arange(t)
    outs = []
    for gi, w in enumerate(POOL_WINDOWS):
        pg = pf[..., gi * POOL_GROUP_DIM:(gi + 1) * POOL_GROUP_DIM]
        cs = jnp.concatenate([jnp.zeros((b, 1, POOL_GROUP_DIM), jnp.float32), jnp.cumsum(pg, axis=1)], axis=1)
        lo = jnp.clip(pos - w // 2, 0, t)
        hi = jnp.clip(pos + w - w // 2, 0, t)
        mean = (cs[:, hi] - cs[:, lo]) / (hi - lo).astype(jnp.float32)[:, None]
        outs.append(mean - pg)
    return jnp.concatenate(outs, axis=-1).astype(p.dtype)


def _heads(z):
    return z.reshape(z.shape[:-1] + (N_HEADS_A, HEAD_DIM))


def _to_scan(z):
    z = jnp.concatenate([z[:1], jnp.flip(z[1:], axis=2)], axis=0)
    return jnp.moveaxis(z, 2, 0)


def _from_scan(y):
    y = jnp.moveaxis(y, 0, 2)
    return y[0] + jnp.flip(y[1], axis=1)


def _rwkv7_step(S, inp):
    r, w, k, v, kk, a = inp
    sa = jnp.einsum('dbhvk,dbhk->dbhv', S, kk)
    S = S * w[..., None, :] - sa[..., :, None] * (kk * a)[..., None, :] + v[..., :, None] * k[..., None, :]
    return S, jnp.einsum('dbhvk,dbhk->dbhv', S, r)


def _token_mixer(h, s0, latent, p):
    f32 = jnp.float32
    proj = h @ p['w_mix_in']
    ps = proj[..., :SHIFT_W]
    pool_in = proj[..., SHIFT_W:SHIFT_W + D_B]
    gate_logits = proj[..., SHIFT_W + D_B:]
    shifted = _shift_grid(ps) if latent else _shift_seq(ps)
    ps = (ps + p['mu_shift'] * (shifted - ps)).astype(f32)
    b, t, _ = ps.shape
    o1, o2, o3 = D_A, 2 * D_A, 3 * D_A
    o4 = o3 + 2 * LORA_W
    o5 = o4 + 2 * LORA_A
    r, k, v = ps[..., :o1], ps[..., o1:o2], ps[..., o2:o3]
    w_down = ps[..., o3:o4].reshape(b, t, 2, LORA_W)
    a_down = ps[..., o4:o5].reshape(b, t, 2, LORA_A)
    g_down = ps[..., o5:]
    w_raw = p['w0'][:, None, None, :] + jnp.einsum('btdr,drc->dbtc', jnp.tanh(w_down), p['w_up'])
    decay = jnp.exp(-jnp.exp(-jax.nn.softplus(-w_raw) - 0.5))
    a = jax.nn.sigmoid(p['a0'][:, None, None, :] + jnp.einsum('btdr,drc->dbtc', a_down, p['a_up']))
    g = jax.nn.sigmoid(g_down) @ p['g_up']
    kk = _heads(k * p['k_k'])
    kk = kk / jnp.maximum(jnp.sqrt(jnp.sum(kk * kk, -1, keepdims=True)), 1e-12)
    k_dir = k[None] * (1.0 + (a - 1.0) * p['k_a'])
    r_h, k_h, v_h = _heads(r), _heads(k), _heads(v)
    both = lambda z: jnp.broadcast_to(z[None], (2,) + z.shape)
    xs = (_to_scan(both(r_h)), _to_scan(_heads(decay)), _to_scan(_heads(k_dir)),
          _to_scan(both(v_h)), _to_scan(both(kk)), _to_scan(_heads(a)))
    s_final, y = lax.scan(_rwkv7_step, s0.astype(f32), xs)
    y = _from_scan(y)
    mu = jnp.mean(y, -1, keepdims=True)
    var = jnp.mean(jnp.square(y - mu), -1, keepdims=True)
    y = ((y - mu) * lax.rsqrt(var + GN_EPS)).reshape(b, t, D_A) * p['lnx_g'] + p['lnx_b']
    bonus = jnp.sum(r_h * k_h * p['r_k'], -1, keepdims=True) * v_h
    y = (y + bonus.reshape(b, t, D_A)) * g
    y_a = y.astype(h.dtype) @ p['w_o_rwkv']
    u = _centred_pool_residual(pool_in).reshape(b, t, POOL_GROUPS, POOL_GROUP_DIM)
    u = jnp.einsum('btgc,gcd->btgd', u, p['w_pool']).reshape(b, t, D_B) * p['pool_scale']
    y_b = u @ p['w_o_pool']
    gate_a = jax.nn.sigmoid(gate_logits[..., :D_MODEL])
    gate_b = jax.nn.sigmoid(gate_logits[..., D_MODEL:])
    return (gate_a * y_a + gate_b * y_b) @ p['w_out'], s_final


def _trunk_layer(x, mod, s0, latent, p):
    m = [mod[:, :, i, :] for i in range(N_MOD)]
    h = x * (1.0 + m[1]) + m[0]
    x = _layer_norm(ALPHA * x + 0.5 * m[2] * _swiglu(h, p['ffn_in'][0], p['ffn_out'][0]), p['ln_g'][0], p['ln_b'][0])
    h = x * (1.0 + m[4]) + m[3]
    mix, s_final = _token_mixer(h, s0, latent, p)
    x = _layer_norm(ALPHA * x + m[5] * mix, p['ln_g'][1], p['ln_b'][1])
    h = x * (1.0 + m[7]) + m[6]
    x = _layer_norm(ALPHA * x + 0.5 * m[8] * _swiglu(h, p['ffn_in'][1], p['ffn_out'][1]), p['ln_g'][2], p['ln_b'][2])
    return x, s_final


def setup_inputs(seed: int = 0) -> dict:
    key = jax.random.key(seed)
    ks = jax.random.split(key, 32)
    f32 = jnp.float32
    nrm = lambda k, shape, scale: jax.random.normal(k, shape, f32) * scale
    return {
        'x_prompt': nrm(ks[0], (BATCH, SEQ, D_MODEL), 1.0),
        'x_sample': nrm(ks[1], (DEC_BATCH, DEC_SEQ, D_MODEL), 1.0),
        'c': nrm(ks[2], (DEC_BATCH, D_MODEL), 1.0),
        'state_rwkv': nrm(ks[3], (DEC_BATCH, DEPTH, 2, N_HEADS_A, HEAD_DIM, HEAD_DIM), 0.5),
        'c_ctx': nrm(ks[4], (D_MODEL,), 1.0),
        'w_mod': nrm(ks[5], (DEPTH, D_MODEL, N_MOD * D_MODEL), D_MODEL ** -0.5),
        'b_mod': nrm(ks[6], (DEPTH, N_MOD * D_MODEL), 0.01),
        'ln_g': 1.0 + nrm(ks[7], (DEPTH, 3, D_MODEL), 0.05),
        'ln_b': nrm(ks[8], (DEPTH, 3, D_MODEL), 0.02),
        'ffn_in': nrm(ks[9], (DEPTH, 2, D_MODEL, 2 * D_FF), D_MODEL ** -0.5),
        'ffn_out': nrm(ks[10], (DEPTH, 2, D_FF, D_MODEL), BETA * D_FF ** -0.5),
        'w_mix_in': nrm(ks[11], (DEPTH, D_MODEL, MIX_IN), D_MODEL ** -0.5),
        'mu_shift': jax.random.uniform(ks[12], (DEPTH, SHIFT_W), f32),
        'w0': nrm(ks[13], (DEPTH, 2, D_A), 0.5),
        'w_up': nrm(ks[14], (DEPTH, 2, LORA_W, D_A), LORA_W ** -0.5),
        'a0': nrm(ks[15], (DEPTH, 2, D_A), 0.5),
        'a_up': nrm(ks[16], (DEPTH, 2, LORA_A, D_A), LORA_A ** -0.5),
        'g_up': nrm(ks[17], (DEPTH, LORA_G, D_A), LORA_G ** -0.5),
        'k_k': 0.85 + nrm(ks[18], (DEPTH, D_A), 0.05),
        'k_a': 1.0 + nrm(ks[19], (DEPTH, D_A), 0.05),
        'r_k': nrm(ks[20], (DEPTH, N_HEADS_A, HEAD_DIM), 0.1),
        'lnx_g': 1.0 + nrm(ks[21], (DEPTH, D_A), 0.05),
        'lnx_b': nrm(ks[22], (DEPTH, D_A), 0.02),
        'w_o_rwkv': nrm(ks[23], (DEPTH, D_A, D_MODEL), BETA * D_A ** -0.5),
        'w_pool': nrm(ks[24], (DEPTH, POOL_GROUPS, POOL_GROUP_DIM, POOL_GROUP_DIM), POOL_GROUP_DIM ** -0.5),
        'pool_scale': 1.0 + nrm(ks[25], (DEPTH, D_B), 0.05),
        'w_o_pool': nrm(ks[26], (DEPTH, D_B, D_MODEL), BETA * D_B ** -0.5),
        'w_out': nrm(ks[27], (DEPTH, D_MODEL, D_MODEL), BETA * D_MODEL ** -0.5),
    }


def reference(x_prompt, x_sample, c, state_rwkv, c_ctx, w_mod, b_mod, ln_g, ln_b, ffn_in, ffn_out,
              w_mix_in, mu_shift, w0, w_up, a0, a_up, g_up, k_k, k_a, r_k, lnx_g, lnx_b,
              w_o_rwkv, w_pool, pool_scale, w_o_pool, w_out):
    y_p = x_prompt
    y_s = x_sample
    ctx_states = []
    for l in range(DEPTH):
        p = {
            'ln_g': ln_g[l], 'ln_b': ln_b[l], 'ffn_in': ffn_in[l], 'ffn_out': ffn_out[l],
            'w_mix_in': w_mix_in[l], 'mu_shift': mu_shift[l], 'w0': w0[l], 'w_up': w_up[l],
            'a0': a0[l], 'a_up': a_up[l], 'g_up': g_up[l], 'k_k': k_k[l], 'k_a': k_a[l],
            'r_k': r_k[l], 'lnx_g': lnx_g[l], 'lnx_b': lnx_b[l], 'w_o_rwkv': w_o_rwkv[l],
            'w_pool': w_pool[l], 'pool_scale': pool_scale[l], 'w_o_pool': w_o_pool[l], 'w_out': w_out[l],
        }
        mod_ctx = _modulation(c_ctx[None, :], w_mod[l], b_mod[l])
        s0_ctx = jnp.zeros((2, y_p.shape[0], N_HEADS_A, HEAD_DIM, HEAD_DIM), jnp.float32)
        y_p, s_ctx = _trunk_layer(y_p, mod_ctx, s0_ctx, False, p)
        ctx_states.append(jnp.moveaxis(s_ctx, 0, 1))
        mod_lat = _modulation(c, w_mod[l], b_mod[l])
        s0_lat = jnp.moveaxis(state_rwkv[:, l], 1, 0)
        y_s, _ = _trunk_layer(y_s, mod_lat, s0_lat, True, p)
    new_state_rwkv = jnp.stack(ctx_states, axis=1).astype(x_prompt.dtype)
    return (y_p, y_s, new_state_rwkv)
```

```python
from contextlib import ExitStack
import numpy as np
import concourse.bass as bass
import concourse.mybir as mybir
from concourse.bass_utils import run_bass_kernel_spmd

F32 = mybir.dt.float32
BF16 = mybir.dt.bfloat16
I32 = mybir.dt.int32
AF = mybir.ActivationFunctionType
ALU = mybir.AluOpType

D = 1024
NP_ = 512
NS_ = 1024
NT = NP_ + NS_
DFF = 2816
ALPHA = 2.0 ** 0.25
LN_EPS = 1e-5
GN_EPS = 64e-5
STAGE = 4


class _Op:
    __slots__ = ("eng", "fn", "waits", "kind", "sem", "val")


class Prog:
    ENG = ("pe", "act", "dve", "pool", "sp")
    K = 6

    def __init__(self, nc):
        self.nc = nc
        self.ops = {e: [] for e in self.ENG}
        self.count = {e: 0 for e in self.ENG}
        self.known = {e: {} for e in self.ENG}
        self.last_w = {}
        self.rd_eng = {}
        self.rd_dma = {}
        self.dma_n = {e: 0 for e in self.ENG}
        self.ncc = 0
        self.dma_final = {}
        self.muted = False

    def _need(self, eng, d, waits):
        if d is None:
            return
        if d.kind == "c" and d.eng == eng and eng == "pe":
            return
        if self.known[eng].get(d.sem, 0) >= d.val:
            return
        if waits.get(d.sem, 0) < d.val:
            waits[d.sem] = d.val

    def op(self, eng, fn, reads=(), writes=(), kind="c"):
        if self.muted:
            return None
        r2, w2 = [], []
        for k in reads:
            if isinstance(k, tuple) and k[0] in ("bank", "bankc"):
                w2.append(("bank", k[1]))
            else:
                r2.append(k)
        for k in writes:
            if isinstance(k, tuple) and k[0] in ("bank", "bankc"):
                k = ("bank", k[1])
            if k not in w2:
                w2.append(k)
        reads, writes = r2, w2
        o = _Op()
        o.eng = eng
        o.fn = fn
        o.kind = kind
        waits = {}
        for k in reads:
            self._need(eng, self.last_w.get(k), waits)
        for k in writes:
            self._need(eng, self.last_w.get(k), waits)
            for d in self.rd_eng.get(k, {}).values():
                self._need(eng, d, waits)
            for d in self.rd_dma.get(k, ()):
                self._need(eng, d, waits)
        if kind == "d":
            i = self.dma_n[eng]
            o.sem = ("dma", eng, i % self.K)
            o.val = 16 * (i // self.K + 1)
            if i >= self.K and self.known[eng].get(o.sem, 0) < o.val - 16:
                waits[o.sem] = max(waits.get(o.sem, 0), o.val - 16)
            self.dma_n[eng] += 1
            self.dma_final[o.sem] = o.val
        elif kind == "cc":
            o.sem = ("cc", self.ncc)
            o.val = 1
            self.ncc += 1
        else:
            self.count[eng] += 1
            o.sem = ("eng", eng)
            o.val = self.count[eng]
        for k, v in waits.items():
            self.known[eng][k] = v
        o.waits = list(waits.items())
        for k in reads:
            if kind == "d":
                self.rd_dma.setdefault(k, []).append(o)
            else:
                self.rd_eng.setdefault(k, {})[eng] = o
        for k in writes:
            self.last_w[k] = o
            self.rd_eng[k] = {}
            self.rd_dma[k] = []
        self.ops[eng].append(o)
        return o

    def barrier(self):
        if self.muted:
            return
        targets = {}
        for e in self.ENG:
            if self.count[e]:
                targets[("eng", e)] = self.count[e]
        targets.update(self.dma_final)
        for e in self.ENG:
            o = _Op()
            o.eng = e
            o.fn = None
            o.kind = "w"
            w = {}
            for k, v in targets.items():
                if k == ("eng", e):
                    continue
                if self.known[e].get(k, 0) < v:
                    w[k] = v
                    self.known[e][k] = v
            o.waits = list(w.items())
            self.ops[e].append(o)

    def emit(self):
        nc = self.nc
        with ExitStack() as es:
            sems = {}
            for e in self.ENG:
                sems[("eng", e)] = es.enter_context(nc.semaphore("s_" + e))
                for i in range(min(self.K, self.dma_n[e])):
                    sems[("dma", e, i)] = es.enter_context(nc.semaphore("d_%s%d" % (e, i)))
            for i in range(self.ncc):
                sems[("cc", i)] = es.enter_context(nc.semaphore("cc%d" % i))
            self.barrier()
            block = es.enter_context(nc.Block())

            def run(ename):
                def body(e):
                    for o in self.ops[ename]:
                        for k, v in o.waits:
                            e.wait_ge(sems[k], v)
                        if o.fn is None:
                            continue
                        ins = o.fn(e)
                        if o.kind == "d":
                            ins.then_inc(sems[o.sem], 16)
                        elif o.kind == "cc":
                            ins.then_inc(sems[o.sem])
                        else:
                            ins.then_inc(sems[o.sem], 1)
                return body

            block.tensor(run("pe"))
            block.scalar(run("act"))
            block.vector(run("dve"))
            block.gpsimd(run("pool"))
            block.sync(run("sp"))


def _fm(v):
    v = np.asarray(v, np.float32).reshape(-1, 128)
    return np.ascontiguousarray(v.T)


class VecPack:
    def __init__(self):
        self.cols = {}
        self.n = 0
        self.parts = []

    def add(self, name, arr):
        arr = np.asarray(arr, np.float32)
        assert arr.shape[0] == 128
        arr = arr.reshape(128, -1)
        self.cols[name] = (self.n, arr.shape[1])
        self.n += arr.shape[1]
        self.parts.append(arr)

    def build(self):
        return np.ascontiguousarray(np.concatenate(self.parts, axis=1))


POOL_W = (2, 4, 8, 16)


def _pool_scl(T, left_edge, right_edge):
    L = np.zeros((4, 8), np.float32)
    R = np.zeros((4, 8), np.float32)
    for gi, w in enumerate(POOL_W):
        for i in range(8):
            t = i
            cnt = (min(t + w - w // 2, T) - max(t - w // 2, 0)) if left_edge else w
            L[gi, i] = 1.0 / cnt
            t = T - 8 + i
            cnt = (min(t + w - w // 2, T) - max(t - w // 2, 0)) if right_edge else w
            R[gi, i] = 1.0 / cnt
    return L, R


def _const_pack():
    cp = VecPack()
    cp.add("ident", np.eye(128, dtype=np.float32))
    cp.add("ones_ln", np.full((128, 128), 1.0 / 1024.0, np.float32))
    blk = np.kron(np.eye(2, dtype=np.float32), np.ones((64, 64), np.float32))
    cp.add("blk1", blk)
    cp.add("blk64", blk / 64.0)
    s = np.arange(64)
    strict_f = (s[:, None] < s[None, :]).astype(np.float32)
    incl_f = (s[:, None] <= s[None, :]).astype(np.float32)
    t2 = lambda m: np.tile(m, (2, 2))
    cp.masks = np.concatenate([np.concatenate([t2(strict_f), t2(incl_f), t2(strict_f), t2(incl_f)], axis=1),
                               np.concatenate([t2(strict_f.T), t2(incl_f.T), t2(strict_f.T), t2(incl_f.T)], axis=1),
                               t2(strict_f.T), t2(strict_f)], axis=1).astype(np.float32)
    p = np.arange(128)
    for name, m, r in (("mP", 2, 0), ("mN", 2, 1), ("mL", 4, 0), ("mR", 4, 1), ("mU", 4, 2), ("mD", 4, 3)):
        cp.add(name, (p % m == r).astype(np.float32)[:, None])
    L, R = _pool_scl(256, True, True)
    cp.add("psclL_p", np.tile(L.reshape(1, 32), (128, 1)))
    cp.add("psclR_p", np.tile(R.reshape(1, 32), (128, 1)))
    return cp


def _mix_cols(pairs):
    cols = list(range(3072, 3456))
    for pr in pairs:
        for base in (0, 1024, 2048):
            cols += list(range(base + pr * 128, base + pr * 128 + 128))
    return np.array(cols)


def _vec_layout(inputs, core):
    b, q = core // 4, core % 4
    g = lambda k: np.asarray(inputs[k][0], np.float32)
    vp = VecPack()
    vp.add("b_mod", _fm(g("b_mod")))
    vp.add("ln_g", _fm(g("ln_g").reshape(-1)))
    vp.add("ln_b", _fm(g("ln_b").reshape(-1)))
    mu = g("mu_shift")
    vp.add("mu_pp", _fm(mu[_mix_cols(range(8))]))
    vp.add("mu_ss", _fm(mu[_mix_cols([2 * q, 2 * q + 1])]))
    for nm in ("k_k", "k_a", "lnx_g", "lnx_b"):
        v = _fm(g(nm))
        vp.add(nm, v)
        vp.add(nm + "_s", v[:, 2 * q:2 * q + 2])
    v = _fm(g("r_k").reshape(-1))
    vp.add("r_k", v)
    vp.add("r_k_s", v[:, 2 * q:2 * q + 2])
    for nm in ("w0", "a0"):
        v = np.stack([_fm(g(nm)[d]) for d in range(2)], axis=1)
        vp.add(nm, v.reshape(128, 16))
        vp.add(nm + "_s", np.ascontiguousarray(v[:, :, 2 * q:2 * q + 2]).reshape(128, 4))
    vp.add("pool_scale", _fm(g("pool_scale")))
    L, R = _pool_scl(4096, q == 0, False)
    vp.add("psclL_s1", np.tile(L.reshape(1, 32), (128, 1)))
    vp.add("psclR_s1", np.tile(R.reshape(1, 32), (128, 1)))
    L2, R2 = _pool_scl(4096, False, q == 3)
    vp.add("psclL_s2", np.tile(L2.reshape(1, 32), (128, 1)))
    vp.add("psclR_s2", np.tile(R2.reshape(1, 32), (128, 1)))
    ohL = np.zeros((128, 4), np.float32)
    ohR = np.zeros((128, 4), np.float32)
    if q > 0:
        ohL[:, q - 1] = 1.0
    if q < 3:
        ohR[:, q + 1] = 1.0
    vp.add("ohL", ohL)
    vp.add("ohR", ohR)
    return vp


DEBUG_STOP = None
LAM = 0.6065306597126334
AR_F32 = 26624


def build_program(vcols, ccols, nvec, ncon, stage=STAGE):
    nc = bass.Bass("TRN2", target_bir_lowering=False)
    P = Prog(nc)
    es = ExitStack()

    def dram_in(name, shape, dt=F32):
        return nc.dram_tensor(name, list(shape), dt, kind="ExternalInput").ap()

    def dram_out(name, shape, dt=F32):
        return nc.dram_tensor(name, list(shape), dt, kind="ExternalOutput").ap()

    def sb(name, shape, dt):
        return es.enter_context(nc.sbuf_tensor(name, list(shape), dt))

    xin = dram_in("xin", [NT, D])
    cvec = dram_in("cvec", [128, 16])
    vecs_d = dram_in("vecs", [128, nvec])
    cons_d = dram_in("cons", [128, ncon])
    cmask_d = dram_in("cmask", [128, 1280])
    w_mod = dram_in("w_mod", [D, 9 * D])
    ffn_in = dram_in("ffn_in", [2, D, 2 * DFF])
    ffn_out = dram_in("ffn_out", [2, DFF, D])
    w_mix_pp = dram_in("w_mix_pp", [D, 27 * 128])
    w_mix_ss = dram_in("w_mix_ss", [D, 9 * 128])
    w_mix_pg = dram_in("w_mix_pg", [D, 20 * 128])
    wup_d = dram_in("wup", [128, D])
    aup_d = dram_in("aup", [128, D])
    gup_d = dram_in("gup", [128, D])
    wup_sd = dram_in("wup_s", [128, 256])
    aup_sd = dram_in("aup_s", [128, 256])
    gup_sd = dram_in("gup_s", [128, 256])
    worw_d = dram_in("w_o_rwkv", [D, D])
    wpool_d = dram_in("w_pool", [512, 128])
    wopool_d = dram_in("w_o_pool", [512, D])
    wout_d = dram_in("w_out", [D, D])
    s0T_d = dram_in("s0T", [512, 64])
    qidx_d = dram_in("qidx", [1, 4], I32)
    y_out = dram_out("y_out", [NT, D])
    st_out = dram_out("st_out", [2 * 2 * 16 * 64, 64])
    xsave = nc.dram_tensor("xsave", [128, 8 * NT], F32).ap()
    ag1_in = [nc.dram_tensor("ag1_in%d" % i, [512, NS_], BF16).ap() for i in range(2)]
    ag1_out = [nc.dram_tensor("ag1_out%d" % i, [4 * 512, NS_], BF16).ap() for i in range(2)]
    ag2_in = [nc.dram_tensor("ag2_in%d" % i, [128, 4096], BF16).ap() for i in range(2)]
    ag2_out = [nc.dram_tensor("ag2_out%d" % i, [512, 4096], BF16).ap() for i in range(2)]
    RG = [[0, 1, 2, 3], [4, 5, 6, 7]]

    xT = sb("xT", [128, 8, NT], F32)
    hb = sb("hb", [128, 8 * NT], BF16)
    hT = hb[:, :].rearrange("p (c n) -> p c n", n=NT)
    woB = hb[:, 0:22 * 512].rearrange("p (j n) -> p j n", n=512)
    AR = sb("AR", [128, AR_F32], F32)
    vecs = sb("vecs_s", [128, nvec], F32)
    cons = sb("cons_s", [128, ncon], F32)
    cv = sb("cv", [128, 16], F32)
    cvs = sb("cvs", [128, 16], BF16)
    mod = sb("mod", [128, 9, 8, 2], F32)
    drv = sb("drv", [128, 12, 8, 2], F32)
    xtok = [sb("xtok%d" % i, [128, D], F32) for i in range(2)]
    tmpA = [sb("tmpA%d" % i, [128, 512], F32) for i in range(2)]
    tmpB = [sb("tmpB%d" % i, [128, 512], F32) for i in range(2)]
    lnm = sb("lnm", [128, 512], F32)
    lnr = sb("lnr", [128, 512], F32)
    identb = sb("identb", [128, 128], BF16)
    cmask = sb("cmask_s", [128, 1280], BF16)
    Dbuf = sb("Dbuf", [128, 2, 5, 128], BF16)
    MASKS = {"maskA_f": (0, 512), "maskA_b": (512, 1024), "maskPT_f": (1024, 1152), "maskPT_b": (1152, 1280)}

    def CM(name):
        a_, b_ = MASKS[name]
        return cmask[:, a_:b_]
    banks = [es.enter_context(nc.psum_tensor("bank%d" % i, [128, 512], F32)) for i in range(8)]

    class Arena:
        def __init__(self, base=None, size=None):
            self.off = 0
            self.base = AR if base is None else base
            self.size = AR_F32 if size is None else size

        def f32(self, n):
            a = self.base[:, self.off:self.off + n]
            self.off += n
            assert self.off <= self.size, self.off
            return a

        def bf(self, n):
            assert n % 2 == 0
            a = self.base[:, self.off:self.off + n // 2].bitcast(BF16)
            self.off += n // 2
            assert self.off <= self.size, self.off
            return a

    ar = Arena()
    act = ar.bf(22 * NT).rearrange("p (j n) -> p j n", n=NT)
    woA = ar.bf(22 * 512).rearrange("p (j n) -> p j n", n=512)
    win = [[ar.bf(8 * 256).rearrange("p (k n) -> p k n", n=256) for g in range(2)] for i in range(2)]
    actf = act.rearrange("p j n -> p (j n)")
    wm = [actf[:, i * 8192:(i + 1) * 8192].rearrange("p (k n) -> p k n", n=1024) for i in range(2)]

    def V(name, a=0, b=None):
        o, n = vcols[name]
        b = n if b is None else b
        return vecs[:, o + a:o + b]

    def C(name, a=0, b=None):
        o, n = ccols[name]
        b = n if b is None else b
        return cons[:, o + a:o + b]

    def mm(out, lhsT, rhs, start, stop, reads, writes):
        P.op("pe", lambda e: e.matmul(out, lhsT, rhs, start=start, stop=stop), reads=reads, writes=writes)

    def a_copy(dst, src, reads, writes, scale=None):
        if scale is None:
            P.op("act", lambda e: e.copy(dst, src), reads=reads, writes=writes)
        else:
            P.op("act", lambda e: e.activation(dst, src, AF.Identity, scale=scale), reads=reads, writes=writes)

    def a_act(dst, src, func, reads, writes, bias=None, scale=None):
        kw = {}
        if bias is not None:
            kw["bias"] = bias
        if scale is not None:
            kw["scale"] = scale
        P.op("act", lambda e: e.activation(dst, src, func, **kw), reads=reads, writes=writes)

    def v_copy(dst, src, reads, writes, eng="dve"):
        P.op(eng, lambda e: e.tensor_copy(dst, src), reads=reads, writes=writes)

    def v_tt(dst, a, b, op, reads, writes, eng="dve"):
        P.op(eng, lambda e: e.tensor_tensor(dst, a, b, op), reads=reads, writes=writes)

    def v_ts(dst, a, s1, s2, op0, op1, reads, writes, eng="dve"):
        if s2 is None:
            P.op(eng, lambda e: e.tensor_scalar(dst, a, s1, None, op0), reads=reads, writes=writes)
        else:
            P.op(eng, lambda e: e.tensor_scalar(dst, a, s1, s2, op0, op1), reads=reads, writes=writes)

    def v_stt(dst, a, s, b, op0, op1, reads, writes, eng="dve"):
        P.op("dve", lambda e: e.scalar_tensor_tensor(dst, a, s, b, op0, op1), reads=reads, writes=writes)

    def dma(eng, dst, src, reads, writes):
        P.op(eng, lambda e: e.dma_start(out=dst, in_=src), reads=reads, writes=writes, kind="d")

    evc = [0]

    def ev(dst, src, reads, writes):
        evc[0] += 1
        if evc[0] % 2:
            a_copy(dst, src, reads, writes)
        else:
            v_copy(dst, src, reads, writes)

    dma("sp", vecs[:, :], vecs_d, [], ["vecs"])
    dma("sp", cons[:, :], cons_d, [], ["cons"])
    dma("sp", cv[:, :], cvec, [], ["cv"])
    dma("pool", cmask[:, :], cmask_d, [], ["cons"])
    a_act(cvs[:, :], cv[:, :], AF.Silu, ["cv"], ["cvs"])
    v_copy(identb[:, :], C("ident"), ["cons"], ["identb"])

    for i in range(NT // 128):
        xt = xtok[i % 2]
        dma("sp", xt[:, :], xin[i * 128:(i + 1) * 128, :], [], [("xtok", i % 2)])
        for half in range(2):
            bk = banks[6 + half]
            for cc in range(4):
                c = half * 4 + cc
                P.op("pe", lambda e, bk=bk, cc=cc, xt=xt, c=c: e.transpose(
                    bk[:, cc * 128:(cc + 1) * 128], xt[:, c * 128:(c + 1) * 128], C("ident")),
                    reads=[("xtok", i % 2), "cons"], writes=[("bank", 6 + half)])
            dst = xT[:, half * 4:half * 4 + 4, i * 128:(i + 1) * 128]
            src = bk[:, :].rearrange("p (c n) -> p c n", n=128)
            wr = [("xT", c_, i // 4) for c_ in range(half * 4, half * 4 + 4)]
            if half == 0:
                a_copy(dst, src, [("bank", 6)], wr)
            else:
                v_copy(dst, src, [("bank", 7)], wr)

    wmv = w_mod.rearrange("(kc p) n -> p kc n", p=128)
    for i in range(9):
        wt = wm[i % 2]
        dma("pool", wt, wmv[:, :, i * 1024:(i + 1) * 1024], [], [("wm", i % 2)])
        bk = banks[4 + (i % 2)]
        for oc in range(8):
            for kc in range(8):
                mm(bk[:, oc * 2:oc * 2 + 2], wt[:, kc, oc * 128:(oc + 1) * 128], cvs[:, kc * 2:kc * 2 + 2],
                   kc == 0, kc == 7, [("wm", i % 2), "cvs"], [("bank", 4 + (i % 2))])
        v_tt(mod[:, i, :, :], bk[:, 0:16].rearrange("p (c g) -> p c g", g=2),
             V("b_mod", i * 8, i * 8 + 8).unsqueeze(2).to_broadcast([128, 8, 2]), ALU.add,
             [("bank", 4 + (i % 2)), "vecs"], [("mod", i)])

    def lng(l):
        return V("ln_g", l * 8, l * 8 + 8).unsqueeze(2).to_broadcast([128, 8, 2])

    def lnb(l):
        return V("ln_b", l * 8, l * 8 + 8).unsqueeze(2).to_broadcast([128, 8, 2])

    def small(fn, reads, writes):
        P.op("dve", fn, reads=reads, writes=writes)

    small(lambda e: e.tensor_scalar_add(drv[:, 0], mod[:, 1], 1.0), [("mod", 1)], [("drv", 0)])
    small(lambda e: e.tensor_scalar_mul(drv[:, 1], mod[:, 2], 0.5), [("mod", 2)], [("drv", 1)])
    small(lambda e: e.tensor_scalar_add(drv[:, 4], mod[:, 4], 1.0), [("mod", 4)], [("drv", 4)])
    small(lambda e: e.tensor_tensor(drv[:, 2], drv[:, 4], lng(0), ALU.mult), [("drv", 4), "vecs"], [("drv", 2)])
    small(lambda e: e.tensor_tensor(drv[:, 3], drv[:, 4], lnb(0), ALU.mult), [("drv", 4), "vecs"], [("drv", 3)])
    small(lambda e: e.tensor_tensor(drv[:, 3], drv[:, 3], mod[:, 3], ALU.add), [("drv", 3), ("mod", 3)], [("drv", 3)])
    small(lambda e: e.tensor_scalar_add(drv[:, 4], mod[:, 7], 1.0), [("mod", 7), ("drv", 2), ("drv", 3)], [("drv", 4)])
    small(lambda e: e.tensor_tensor(drv[:, 5], drv[:, 4], lng(1), ALU.mult), [("drv", 4), "vecs"], [("drv", 5)])
    small(lambda e: e.tensor_tensor(drv[:, 6], drv[:, 4], lnb(1), ALU.mult), [("drv", 4), "vecs"], [("drv", 6)])
    small(lambda e: e.tensor_tensor(drv[:, 6], drv[:, 6], mod[:, 6], ALU.add), [("drv", 6), ("mod", 6)], [("drv", 6)])
    small(lambda e: e.tensor_scalar_mul(drv[:, 7], mod[:, 8], 0.5), [("mod", 8)], [("drv", 7)])
    for l in range(2):
        small(lambda e, l=l: e.tensor_scalar_mul(drv[:, 8 + 2 * l], lng(l), ALPHA), ["vecs"], [("drv", 8 + 2 * l)])
        small(lambda e, l=l: e.tensor_scalar_mul(drv[:, 9 + 2 * l], lnb(l), ALPHA), ["vecs"], [("drv", 9 + 2 * l)])

    GRP = [(0, 0, NP_), (1, NP_, NT)]
    TT = [(t, t * 512, (t + 1) * 512, 0 if t == 0 else 1) for t in range(3)]
    ALLH = [("hT", c_, g_) for c_ in range(8) for g_ in range(2)]

    def xkeys(c, a, b):
        return [("xT", c, t) for t in range(a // 512, (b + 511) // 512)]

    def modulate(scale_idx, shift_idx):
        for c in range(8):
            for g, a, b in GRP:
                a_act(hT[:, c, a:b], xT[:, c, a:b], AF.Identity, xkeys(c, a, b), [("hT", c, g)],
                      bias=mod[:, shift_idx, c, g:g + 1], scale=drv[:, scale_idx, c, g:g + 1])
            a_act(xT[:, c, :], xT[:, c, :], AF.Identity, xkeys(c, 0, NT), xkeys(c, 0, NT), scale=ALPHA)

    def ffn(f, gate_idx):
        wi = ffn_in[f].rearrange("(kc p) n -> p kc n", p=128)
        wo = ffn_out[f].rearrange("(j p) n -> p j n", p=128)
        dma("pool", woA, wo[:, :, 0:512], [], ["woA"])
        nb = 0
        for jj in range(11):
            wb = win[jj % 2]
            for g2 in range(2):
                dma("pool", wb[g2], wi[:, :, g2 * DFF + jj * 256:g2 * DFF + (jj + 1) * 256], [],
                    [("win", jj % 2, g2)])
            for j2 in range(2):
                j = jj * 2 + j2
                for t, a, b, g in TT:
                    pg = (nb % 2) * 2
                    nb += 1
                    for g2 in range(2):
                        for kc in range(8):
                            mm(banks[pg + g2][:, :], wb[g2][:, kc, j2 * 128:(j2 + 1) * 128], hT[:, kc, a:b],
                               kc == 0, kc == 7, [("win", jj % 2, g2), ("hT", kc, g)], [("bank", pg + g2)])
                    tm = tmpA[nb % 2]
                    a_act(tm[:, :], banks[pg][:, :], AF.Silu, [("bank", pg)], [("tmpA", nb % 2)])
                    v_tt(act[:, j, a:b], tm[:, :], banks[pg + 1][:, :], ALU.mult,
                         [("tmpA", nb % 2), ("bank", pg + 1)], [("act", j, t)])
        dma("pool", woB, wo[:, :, 512:1024], [], ALLH)
        nb = 0
        for oc in range(8):
            wt = woA if oc < 4 else woB
            wkey = ["woA"] if oc < 4 else ALLH
            for t, a, b, g in TT:
                bk = 4 + nb % 2
                nb += 1
                for j in range(22):
                    mm(banks[bk][:, :], wt[:, j, (oc % 4) * 128:(oc % 4 + 1) * 128], act[:, j, a:b],
                       j == 0, j == 21, wkey + [("act", j, t)], [("bank", bk)])
                v_stt(xT[:, oc, a:b], banks[bk][:, :], drv[:, gate_idx, oc, g:g + 1], xT[:, oc, a:b],
                      ALU.mult, ALU.add, [("bank", bk), ("xT", oc, t)], [("xT", oc, t)])

    def run_chains(chains, carry=()):
        gens = [c() for c in chains]
        carry = list(carry)
        while gens:
            for g_ in list(gens):
                try:
                    next(g_)
                except StopIteration:
                    gens.remove(g_)
            for g_ in list(carry):
                try:
                    next(g_)
                except StopIteration:
                    carry.remove(g_)
        return carry

    def layer_norm(outs, tiles=TT):
        P.barrier()
        la = Arena()
        tmps = [{"m": la.f32(512), "r": la.f32(512), "q": [la.f32(512), la.f32(512)]} for _ in tiles]

        def tile_chain(ti, t, a, b, g):
            tm_, bm, br = tmps[ti], 2 * ti, 2 * ti + 1
            lnm_, lnr_ = tm_["m"], tm_["r"]
            km, kr = ("lnm", ti), ("lnr", ti)
            for c in range(8):
                mm(banks[bm][:, :], C("ones_ln"), xT[:, c, a:b], c == 0, c == 7, ["cons", ("xT", c, t)], [("bank", bm)])
            yield
            for c in range(8):
                tq_ = tm_["q"][c % 2]
                a_act(tq_, xT[:, c, a:b], AF.Square, [("xT", c, t)], [("lnq", ti, c % 2)])
                yield
                mm(banks[br][:, :], C("ones_ln"), tq_, c == 0, c == 7, ["cons", ("lnq", ti, c % 2)], [("bank", br)])
                yield
            a_copy(lnm_, banks[bm][:, :], [("bank", bm)], [km]); yield
            v_tt(lnr_, lnm_, lnm_, ALU.mult, [km], [kr]); yield
            v_tt(lnr_, banks[br][:, :], lnr_, ALU.subtract, [kr, ("bank", br)], [kr]); yield
            v_ts(lnr_, lnr_, LN_EPS, None, ALU.add, None, [kr], [kr]); yield
            P.op("act", lambda e: e.sqrt(lnr_, lnr_), reads=[kr], writes=[kr]); yield
            P.op("dve", lambda e: e.reciprocal(lnr_, lnr_), reads=[kr], writes=[kr]); yield
            for c in range(8):
                v_tt(xT[:, c, a:b], xT[:, c, a:b], lnm_, ALU.subtract, [("xT", c, t), km], [("xT", c, t)]); yield
                v_tt(xT[:, c, a:b], xT[:, c, a:b], lnr_, ALU.mult, [("xT", c, t), kr], [("xT", c, t)], eng="pool"); yield
                for (eng, dst_fn, sc_fn, bi_fn, wr_fn) in outs:
                    if eng == "act":
                        a_act(dst_fn(c, a, b), xT[:, c, a:b], AF.Identity, [("xT", c, t)], wr_fn(c, t, g),
                              bias=bi_fn(c, g), scale=sc_fn(c, g))
                    else:
                        v_ts(dst_fn(c, a, b), xT[:, c, a:b], sc_fn(c, g), bi_fn(c, g), ALU.mult, ALU.add,
                             [("xT", c, t)], wr_fn(c, t, g), eng=eng)
                    yield

        run_chains([(lambda ti=ti, tl=tl: tile_chain(ti, *tl)) for ti, tl in enumerate(tiles)])
        P.barrier()

    def ln_outs(l, a_idx, b_idx):
        return [
            ("act", lambda c, a, b: hT[:, c, a:b], lambda c, g: drv[:, a_idx, c, g:g + 1],
             lambda c, g: drv[:, b_idx, c, g:g + 1], lambda c, t, g: [("hT", c, g)]),
            ("act", lambda c, a, b: xT[:, c, a:b], lambda c, g: drv[:, 8 + 2 * l, c, 0:1],
             lambda c, g: drv[:, 9 + 2 * l, c, 0:1], lambda c, t, g: [("xT", c, t)]),
        ]

    def ln_final(l):
        return [("act", lambda c, a, b: xT[:, c, a:b], lambda c, g: V("ln_g", l * 8 + c, l * 8 + c + 1),
                 lambda c, g: V("ln_b", l * 8 + c, l * 8 + c + 1), lambda c, t, g: [("xT", c, t)])]

    def write_out(dst, col0, ntok):
        for i in range(ntok // 128):
            a = col0 + i * 128
            xt = xtok[i % 2]
            for half in range(2):
                bk = banks[6 + half]
                for cc in range(4):
                    c = half * 4 + cc
                    P.op("pe", lambda e, bk=bk, cc=cc, c=c, a=a: e.transpose(
                        bk[:, cc * 128:(cc + 1) * 128], xT[:, c, a:a + 128], C("ident")),
                        reads=[("xT", c, a // 512), "cons"], writes=[("bank", 6 + half)])
                if half == 0:
                    a_copy(xt[:, 0:512], bk[:, :], [("bank", 6)], [("xtok", i % 2, 0)])
                else:
                    v_copy(xt[:, 512:1024], bk[:, :], [("bank", 7)], [("xtok", i % 2, 1)])
            dma("sp", dst[i * 128:(i + 1) * 128, :], xt[:, :], [("xtok", i % 2, 0), ("xtok", i % 2, 1)], [])

    def finish():
        P.emit()
        es.close()
        return nc

    def checkpoint(name):
        if DEBUG_STOP == name:
            P.muted = True

    P.barrier()
    modulate(0, 0)
    ffn(0, 1)
    if stage == 1:
        layer_norm(ln_final(0))
        P.barrier()
        write_out(y_out, 0, NT)
        return finish()
    layer_norm(ln_outs(0, 2, 3))
    P.barrier()
    if stage >= 3:
        for hf in range(2):
            dma("sp", ag1_in[hf].rearrange("(c p) t -> p c t", p=128), hT[:, 4 * hf:4 * hf + 4, NP_:NT],
                [("hT", c_, 1) for c_ in range(8)], [("ag1in", hf)])
            P.op("pool", lambda e, hf=hf: e.collective_compute("AllGather", ALU.bypass, replica_groups=RG,
                                                               ins=[ag1_in[hf].opt()], outs=[ag1_out[hf].opt()]),
                 reads=[("ag1in", hf)], writes=[("ag1", hf)], kind="cc")

    TP = [tmpA[0][:, :], tmpA[1][:, :], tmpB[0][:, :], tmpB[1][:, :], lnm[:, :], lnr[:, :],
          xtok[0][:, 0:512], xtok[0][:, 512:1024], xtok[1][:, 0:512], xtok[1][:, 512:1024]]
    TPK = [("tp", i) for i in range(10)]

    ar = Arena()
    yfin_p = ar.bf(8 * 512).rearrange("p (c n) -> p c n", n=512)
    yfs_off = ar.off
    yfin_s = ar.bf(8 * 1024).rearrange("p (c n) -> p c n", n=1024)
    yfwd = yfin_s.rearrange("p c n -> p (c n)").rearrange("p (c n) -> p c n", n=4096)
    ar_scan0 = ar.off

    def BK(b):
        return [("bankc", b, i_) for i_ in range(4)]

    def scan_buffers(ar, ntset_main=2):
        sbuf = {}
        sbuf["wq"] = [ar.bf(8 * 384).rearrange("p (k n) -> p k n", n=384) for _ in range(2)]
        sbuf["raw"] = ar.bf(3 * 640).rearrange("p (c n) -> p c n", n=640)
        sbuf["mixd"] = ar.f32(3 * 512).rearrange("p (c n) -> p c n", n=512)
        sbuf["lor"] = ar.bf(3 * 512).rearrange("p (c n) -> p c n", n=512)
        sbuf["vbd"] = ar.bf(8 * 128).rearrange("p (c n) -> p c n", n=128)
        sbuf["vtok"] = [ar.bf(8 * 128).rearrange("p (c n) -> p c n", n=128) for _ in range(2)]
        sbuf["slot"] = []
        for s in range(2):
            d = {}
            for nm in ("Kt", "Bt"):
                d[nm] = ar.bf(8 * 128).rearrange("p (c n) -> p c n", n=128)
            d["KR"] = ar.bf(8 * 256).rearrange("p (c n) -> p c n", n=256)
            d["Kp"], d["Rt"] = d["KR"][:, :, 0:128], d["KR"][:, :, 128:256]
            d["ydir"] = ar.f32(512)
            d["gam"] = ar.f32(8)
            d["cse"] = ar.f32(16)
            sbuf["slot"].append(d)

        def tset(a_):
            d = {}
            d["Am"] = [a_.bf(512) for _ in range(2)]
            d["PTm"] = a_.bf(128)
            d["SS"] = [a_.bf(256) for _ in range(2)]
            d["Nb"] = [a_.bf(128) for _ in range(2)]
            d["Xs"] = a_.bf(128)
            d["Un"] = a_.bf(128)
            d["T0"] = [a_.bf(128) for _ in range(2)]
            d["KB"] = [a_.bf(256) for _ in range(2)]
            d["Ktok"] = [kb_[:, 0:128] for kb_ in d["KB"]]
            d["Btok"] = [kb_[:, 128:256] for kb_ in d["KB"]]
            return d
        sbuf["tset"] = [tset(ar) for _ in range(2)]
        a2 = Arena()
        a2.off = yfs_off
        sbuf["tset"] += [tset(a2) for _ in range(2)]
        sbuf["Tfin"] = a2.f32(128)
        sbuf["TfT"] = a2.f32(128)
        return sbuf

    smc = [0]

    def shift_mix(*a_):
        for _ in shift_mix_g(*a_):
            pass

    def shift_mix_g(raw, ncols_own, own0, coef, nchunk, dst_fn, mode, seg_first, seg_last, keyr, keyw_fn, j0):
        nd = 3 if mode == "seq" else 5
        for k in range(nchunk):
            par = smc[0] % 2
            smc[0] += 1
            cj = j0 + k
            kD = ("Dbuf", par)
            for dd in range(nd):
                a_act(Dbuf[:, par, dd, :], identb[:, :], AF.Identity, ["identb"], [kD], scale=coef[:, cj, dd:dd + 1])
            yield
            bk = 2 + par
            out = banks[bk][:, 0:ncols_own]
            own = raw[:, k, own0:own0 + ncols_own]
            D = lambda dd: Dbuf[:, par, dd, :]
            rd = [kD, keyr]
            mm(out, D(0), own, True, False, rd, BK(bk))
            if mode == "seq":
                o3 = out.rearrange("p (s t) -> p s t", t=256)
                r3 = own.rearrange("p (s t) -> p s t", t=256)
                mm(o3[:, :, 1:256], D(1), r3[:, :, 0:255], False, False, rd, BK(bk))
                mm(o3[:, :, 0:255], D(2), r3[:, :, 1:256], False, True, rd, BK(bk))
            else:
                o3 = out.rearrange("p (s t) -> p s t", t=64)
                r3 = own.rearrange("p (s t) -> p s t", t=64)
                mm(o3[:, :, 1:64], D(1), r3[:, :, 0:63], False, False, rd, BK(bk))
                mm(o3[:, :, 0:63], D(2), r3[:, :, 1:64], False, False, rd, BK(bk))
                lo = 64 if seg_first else 0
                hi = ncols_own - 64 if seg_last else ncols_own
                mm(out[:, lo:ncols_own], D(3), raw[:, k, own0 + lo - 64:own0 + ncols_own - 64], False, False, rd, BK(bk))
                mm(out[:, 0:hi], D(4), raw[:, k, own0 + 64:own0 + hi + 64], False, True, rd, BK(bk))
            yield
            ev(dst_fn(k), out, BK(bk), [keyw_fn(k)])
            yield

    def scan_prep(B, tag, rm, km, vm, lor, vec, wts, vt, bon_i, dir_slots, TP=TP, TPK=TPK, mk=None, extra_chains=()):
        thw, adb = lor[:, 0, :], lor[:, 1, :]

        def K(*a):
            if mk is not None and a[0] in mk:
                return mk[a[0]]
            return (tag,) + a
        kap, bon, tq = TP[5], TP[bon_i], TP[7]
        kkap, kbon, ktq = TPK[5], TPK[bon_i], TPK[7]
        d_first = dir_slots[0][0]
        bV = 4 + (1 - d_first)

        def chain_kappa():
            a_act(tq, km, AF.Identity, [K("km")], [ktq], scale=vec["k_k"]); yield
            v_tt(kap, tq, tq, ALU.mult, [ktq], [kkap]); yield
            mm(banks[6][:, :], C("blk1"), kap, True, True, ["cons", kkap], BK(6)); yield
            v_ts(kap, banks[6][:, :], 1e-24, None, ALU.max, None, BK(6), [kkap]); yield
            P.op("act", lambda e: e.sqrt(kap, kap), reads=[kkap], writes=[kkap]); yield
            P.op("dve", lambda e: e.reciprocal(kap, kap), reads=[kkap], writes=[kkap]); yield
            v_tt(kap, kap, tq, ALU.mult, [kkap, ktq], [kkap]); yield

        def chain_bonus():
            v_stt(bon, rm, vec["r_k"], km, ALU.mult, ALU.mult, [K("rm"), K("km")], [kbon]); yield
            mm(banks[7][:, :], C("blk1"), bon, True, True, ["cons", kbon], BK(7)); yield
            v_tt(bon, banks[7][:, :], vm, ALU.mult, BK(7) + [K("vm")], [kbon]); yield

        vbd, vtok = B["vbd"], B["vtok"][vt]

        def chain_V():
            for h in range(2):
                src = vm[64 * h:64 * h + 64, :].rearrange("p (c t) -> p c t", t=64)
                dst = vbd[64 * h:64 * h + 64, :, 64 * h:64 * h + 64]
                if h == 0:
                    a_copy(dst, src, [K("vm")], [K("vbd", h)])
                else:
                    v_copy(dst, src, [K("vm")], [K("vbd", h)], eng="pool")
                yield
            for half in range(2):
                for cc in range(4):
                    c = half * 4 + cc
                    mm(banks[bV][:, cc * 128:(cc + 1) * 128], vbd[:, c, :], identb[:, :], True, True,
                       [K("vbd", 0), K("vbd", 1), "identb"], BK(bV))
                yield
                ev(vtok[:, half * 4:half * 4 + 4, :], banks[bV][:, :].rearrange("p (c n) -> p c n", n=128),
                   BK(bV), [K("vtok", vt, half)])
                yield

        sg, cs, lex, aa, bb = TP[0], TP[1], TP[2], TP[3], TP[4]
        ksg, kcs, klex, kaa, kbb = TPK[0:5]

        def chain_dir_early(d, sl):
            S = B["slot"][sl]
            KS = lambda *a: (tag, "s", sl) + a
            pb = 4 + d
            mm(banks[pb][:, :], wts["wup"][d], thw[64 * d:64 * d + 64, :], True, True, [K("lor")], BK(pb)); yield
            a_act(sg, banks[pb][:, :], AF.Sigmoid, BK(pb), [ksg], bias=vec["w0"][d]); yield
            mm(banks[pb][:, :], wts["aup"][d], adb[64 * d:64 * d + 64, :], True, True, [K("lor")], BK(pb)); yield
            a_act(aa, banks[pb][:, :], AF.Sigmoid, BK(pb), [kaa], bias=vec["a0"][d]); yield
            P.op("dve", lambda e: e.tensor_tensor_scan(cs, sg, sg, 0.0, ALU.add, ALU.bypass), reads=[ksg], writes=[kcs])
            yield
            cs3 = cs.rearrange("p (c t) -> p c t", t=64)
            cse, gam = S["cse"], S["gam"]
            P.op("pool", lambda e: e.memset(cse[:, 0:1], 0.0), reads=[], writes=[KS("cse0")])
            v_copy(cse[:, 1:9], cs3[:, :, 63], [kcs], [KS("cse")], eng="pool"); yield
            v_tt(gam, cse[:, 1:9], cse[:, 0:8], ALU.subtract, [KS("cse"), KS("cse0")], [KS("gam")], eng="pool"); yield
            a_act(gam, gam, AF.Exp, [KS("gam")], [KS("gam")], scale=-LAM); yield
            if d == 0:
                v_tt(cs3, cs3, cse[:, 0:8].unsqueeze(2).to_broadcast([128, 8, 64]), ALU.subtract,
                     [kcs, KS("cse"), KS("cse0")], [kcs]); yield
                v_tt(lex, cs, sg, ALU.subtract, [kcs, ksg], [klex]); yield
            else:
                lex3 = lex.rearrange("p (c t) -> p c t", t=64)
                v_tt(lex3, cse[:, 1:9].unsqueeze(2).to_broadcast([128, 8, 64]), cs3, ALU.subtract,
                     [kcs, KS("cse")], [klex]); yield
                v_tt(cs, lex, sg, ALU.add, [klex, ksg], [kcs]); yield
            a_act(sg, cs, AF.Exp, [kcs, klex], [ksg], scale=-LAM); yield
            a_act(lex, lex, AF.Exp, [klex], [klex], scale=-LAM); yield
            a_act(cs, cs, AF.Exp, [kcs, ksg], [kcs], scale=LAM); yield

        def bd_build(S, KS, nm, x_, e_, kx, ke, engs):
            for h in range(2):
                dst = S[nm][64 * h:64 * h + 64, :, 64 * h:64 * h + 64]
                a_ = x_[64 * h:64 * h + 64, :].rearrange("p (c t) -> p c t", t=64)
                b_ = e_[64 * h:64 * h + 64, :].rearrange("p (c t) -> p c t", t=64)
                v_tt(dst, a_, b_, ALU.mult, [kx, ke], [KS(nm, h)], eng=engs[h])
                yield

        def dir_late_a(d, sl):
            S = B["slot"][sl]
            KS = lambda *a: (tag, "s", sl) + a
            for _ in bd_build(S, KS, "Rt", rm, sg, K("rm"), ksg, ("pool", "pool")):
                yield
            for _ in bd_build(S, KS, "Kp", kap, lex, kkap, klex, ("pool", "pool")):
                yield

        def dir_late_b(d, sl):
            S = B["slot"][sl]
            KS = lambda *a: (tag, "s", sl) + a
            v_tt(bb, kap, aa, ALU.mult, [kkap, kaa], [kbb]); yield
            v_ts(aa, aa, -1.0, vec["k_a"], ALU.add, ALU.mult, [kaa, kbb], [kaa]); yield
            v_stt(aa, aa, 1.0, km, ALU.add, ALU.mult, [kaa, K("km")], [kaa]); yield
            for _ in bd_build(S, KS, "Bt", bb, cs, kbb, kcs, ("dve", "pool")):
                yield
            for _ in bd_build(S, KS, "Kt", aa, cs, kaa, kcs, ("dve", "pool")):
                yield

        d0, s0 = dir_slots[0]
        carry = [c() for c in extra_chains]
        carry = run_chains([chain_kappa, chain_bonus, chain_V, lambda: chain_dir_early(d0, s0)], carry)
        carry = run_chains([lambda: dir_late_a(d0, s0), lambda: dir_late_b(d0, s0)], carry)
        for (d, sl) in dir_slots[1:]:
            carry = run_chains([lambda: chain_dir_early(d, sl)], carry)
            carry = run_chains([lambda: dir_late_a(d, sl), lambda: dir_late_b(d, sl)], carry)
        for g_ in carry:
            for _ in g_:
                pass

    def scan_chunks(B, tag, groups):
        maxlen = max(len(g["chunks"]) for g in groups)
        for g in groups:
            g["cur"] = 0
        for step in range(maxlen):
            act_g = [g for g in groups if step < len(g["chunks"])]
            ctx = []
            for g in act_g:
                d, sl = g["dir"], g["slot"]
                S, T = B["slot"][sl], B["tset"][g["tset"]]
                x = {"g": g, "S": S, "T": T, "d": d, "c": g["chunks"][step]}
                x["KS"] = (lambda sl: (lambda *a: (tag, "s", sl) + a))(sl)
                x["KT"] = (lambda ts: (lambda *a: (tag, "t", ts) + a))(g["tset"])
                c = x["c"]
                x["Kt"], x["Bt"], x["Kp"], x["Rt"] = S["Kt"][:, c, :], S["Bt"][:, c, :], S["Kp"][:, c, :], S["Rt"][:, c, :]
                for nm in ("Kt", "Bt", "Kp", "Rt"):
                    x["k" + nm] = [x["KS"](nm, 0), x["KS"](nm, 1)]
                x["KR"] = S["KR"][:, c, :]
                x["vtok"] = B["vtok"][g["vt"]][:, c, :]
                x["VT"] = [(tag, "vtok", g["vt"], 0), (tag, "vtok", g["vt"], 1)]
                x["bx"], x["by"] = g["banks"]
                x["msk"] = "f" if d == 0 else "b"
                x["pA"] = step % 2
                ctx.append(x)

            def st_init(x):
                g, T, KT = x["g"], x["T"], x["KT"]
                c = x["c"]
                first = (step == 0) or (g.get("restart") and c in g["restart"])
                if first:
                    cur = g["cur"]
                    T0 = T["T0"][cur]
                    if g["T0init"] is None:
                        P.op("pool", lambda e, T0=T0: e.memset(T0, 0.0), reads=[], writes=[KT("T0", cur)])
                    elif step == 0:
                        g["T0init"](T0, KT("T0", cur))
                x["cur"] = g["cur"]
                x["T0"] = T["T0"][x["cur"]]
                x["kT0"] = KT("T0", x["cur"])

            def st_A(x):
                bx, by = x["bx"], x["by"]
                mm(banks[bx][:, 0:256], x["Kt"], x["KR"], True, True, x["kKt"] + x["kKp"] + x["kRt"], BK(bx))
                mm(banks[bx][:, 256:512], x["Bt"], x["KR"], True, True, x["kBt"] + x["kKp"] + x["kRt"], BK(bx))
                mm(banks[by][:, 0:128], x["Kp"], x["Bt"], True, True, x["kKp"] + x["kBt"], BK(by))
                mm(banks[by][:, 128:256], x["Kt"], identb[:, :], True, True, x["kKt"] + ["identb"], BK(by))
                mm(banks[by][:, 256:384], x["Bt"], identb[:, :], True, True, x["kBt"] + ["identb"], BK(by))

            def st_Aev(x):
                T, KT, pA = x["T"], x["KT"], x["pA"]
                bx, by = x["bx"], x["by"]
                Am = T["Am"][pA]
                x["Am"], x["kAm"] = Am, KT("Am", pA)
                v_tt(Am, banks[bx][:, :], CM("maskA_" + x["msk"]), ALU.mult, BK(bx) + ["cons"], [x["kAm"]])
                v_tt(T["PTm"], banks[by][:, 0:128], CM("maskPT_" + x["msk"]), ALU.mult, BK(by) + ["cons"], [KT("PTm")])
                x["Ktok"], x["Btok"] = T["Ktok"][pA], T["Btok"][pA]
                a_copy(T["KB"][pA], banks[by][:, 128:384], BK(by), [KT("Ktok", pA), KT("Btok", pA)])
                v_tt(T["Nb"][0], identb[:, :], Am[:, 256:384], ALU.subtract, ["identb", x["kAm"]], [KT("Nb", 0)], eng="pool")
                x["Sc"], x["STc"], x["kS"], x["ncur"] = Am[:, 256:384], T["PTm"], [x["kAm"], KT("PTm")], 0

            def mk_series(j):
                def st_S(x):
                    T, KT, bx = x["T"], x["KT"], x["bx"]
                    half = (j % 2) * 256
                    mm(banks[bx][:, half + 128:half + 256], x["Sc"], x["STc"], True, True, x["kS"], BK(bx))
                    if j < 5:
                        mm(banks[bx][:, half:half + 128], x["STc"], x["Sc"], True, True, x["kS"], BK(bx))

                def st_Sev(x):
                    T, KT, bx = x["T"], x["KT"], x["bx"]
                    half = (j % 2) * 256
                    SS = T["SS"][j % 2]
                    kSS = KT("SS", j % 2)
                    if j < 5:
                        a_copy(SS, banks[bx][:, half:half + 256], BK(bx), [kSS])
                    else:
                        a_copy(SS[:, 128:256], banks[bx][:, half + 128:half + 256], BK(bx), [kSS])
                    x["Sc"], x["STc"], x["kS"] = SS[:, 0:128], SS[:, 128:256], [kSS]

                def st_N(x):
                    T, KT, by = x["T"], x["KT"], x["by"]
                    Nprev = T["Nb"][x["ncur"]]
                    if j == 1:
                        mm(banks[by][:, 384:512], identb[:, :], Nprev, True, False, ["identb", KT("Nb", x["ncur"])], BK(by))
                    mm(banks[by][:, 384:512], x["STc"], Nprev, False, (j == 5), x["kS"] + [KT("Nb", x["ncur"])], BK(by))

                def st_Nev(x):
                    T, KT, by = x["T"], x["KT"], x["by"]
                    v_copy(T["Nb"][1 - x["ncur"]], banks[by][:, 384:512], BK(by), [KT("Nb", 1 - x["ncur"])])
                    x["ncur"] = 1 - x["ncur"]
                return [st_S, st_Sev, st_N, st_Nev]

            def st_X(x):
                by = x["by"]
                mm(banks[by][:, 0:128], x["Kp"], x["T0"], True, False, x["kKp"] + [x["kT0"]], BK(by))
                mm(banks[by][:, 0:128], x["Am"][:, 0:128], x["vtok"], False, True, [x["kAm"]] + x["VT"], BK(by))

            def st_Xev(x):
                a_copy(x["T"]["Xs"], banks[x["by"]][:, 0:128], BK(x["by"]), [x["KT"]("Xs")])

            def st_U(x):
                T, KT, bx = x["T"], x["KT"], x["bx"]
                mm(banks[bx][:, 128:256], T["Nb"][x["ncur"]], T["Xs"], True, True, [KT("Nb", x["ncur"]), KT("Xs")], BK(bx))

            def st_Uev(x):
                a_copy(x["T"]["Un"], banks[x["bx"]][:, 128:256], BK(x["bx"]), [x["KT"]("Un")], scale=-1.0)

            def st_T(x):
                T, KT, bx, pA = x["T"], x["KT"], x["bx"], x["pA"]
                mm(banks[bx][:, 256:384], x["Ktok"], x["vtok"], True, False, [KT("Ktok", pA)] + x["VT"], BK(bx))
                mm(banks[bx][:, 256:384], x["Btok"], T["Un"], False, False, [KT("Btok", pA), KT("Un")], BK(bx))
                mm(banks[bx][:, 256:384], identb[:, :], x["T0"], False, True, ["identb", x["kT0"]], BK(bx))
                mm(banks[bx][:, 384:512], x["vtok"], x["Am"][:, 128:256], True, False, x["VT"] + [x["kAm"]], BK(bx))
                mm(banks[bx][:, 384:512], T["Un"], x["Am"][:, 384:512], False, False, [KT("Un"), x["kAm"]], BK(bx))
                mm(banks[bx][:, 384:512], x["T0"], x["Rt"], False, True, [x["kT0"]] + x["kRt"], BK(bx))

            def st_Tev(x):
                g, S, T, KT, KS, bx, c = x["g"], x["S"], x["T"], x["KT"], x["KS"], x["bx"], x["c"]
                cur = x["cur"]
                if g.get("final_at") and c in g["final_at"]:
                    g["final"](c, banks[bx][:, 256:384], ("bank", bx), S["gam"][:, c:c + 1], KS("gam"), x["by"])
                nxt = 1 - cur
                nxt_restart = (step + 1 < len(g["chunks"])) and g.get("restart") and g["chunks"][step + 1] in g["restart"]
                if not nxt_restart:
                    a_copy(T["T0"][nxt], banks[bx][:, 256:384], BK(bx) + [KS("gam")], [KT("T0", nxt)],
                           scale=S["gam"][:, c:c + 1])
                g["ysink"](c, banks[bx], ("bank", bx))
                g["cur"] = nxt

            stages = [st_init, st_A, st_Aev, st_X, st_Xev]
            for j in range(1, 6):
                stages += mk_series(j)
            stages += [st_U, st_Uev, st_T, st_Tev]
            for st in stages:
                for x in ctx:
                    st(x)

    def finalize_y(*a_, **kw):
        for _ in finalize_y_g(*a_, **kw):
            pass

    def finalize_y_g(B, tag, ysum, kysum, bon_key, lor, gup_ap, lnxg, lnxb, dst, kdst, ykeys=None, gkey="wl", bon_i=6,
                     bon_ap=None, tmp=(0, 1, 2), bk=(4, 5), TPx=None, TPKx=None):
        TPx = TP if TPx is None else TPx
        TPKx = TPK if TPKx is None else TPKx
        sgd = lor[:, 2, :]
        bon = TPx[bon_i] if bon_ap is None else bon_ap
        t1, t2, t3 = TPx[tmp[0]], TPx[tmp[1]], TPx[tmp[2]]
        k1, k2, k3 = TPKx[tmp[0]], TPKx[tmp[1]], TPKx[tmp[2]]
        b0, b1 = bk
        yk = ykeys if ykeys is not None else [kysum]
        mm(banks[b0][:, :], C("blk64"), ysum, True, True, ["cons"] + yk, BK(b0)); yield
        v_tt(t1, ysum, ysum, ALU.mult, yk, [k1]); yield
        mm(banks[b1][:, :], C("blk64"), t1, True, True, ["cons", k1], BK(b1)); yield
        a_copy(t2, banks[b0][:, :], BK(b0), [k2]); yield
        v_tt(t3, t2, t2, ALU.mult, [k2], [k3]); yield
        v_tt(t3, banks[b1][:, :], t3, ALU.subtract, BK(b1) + [k3], [k3]); yield
        v_ts(t3, t3, GN_EPS, None, ALU.add, None, [k3], [k3]); yield
        P.op("act", lambda e: e.sqrt(t3, t3), reads=[k3], writes=[k3]); yield
        P.op("dve", lambda e: e.reciprocal(t3, t3), reads=[k3], writes=[k3]); yield
        v_tt(t1, ysum, t2, ALU.subtract, yk + [k2, k1], [k1]); yield
        v_tt(t1, t1, t3, ALU.mult, [k1, k3], [k1]); yield
        a_act(t1, t1, AF.Identity, [k1], [k1], bias=lnxb, scale=lnxg); yield
        v_tt(t1, t1, bon, ALU.add, [k1, bon_key], [k1]); yield
        mm(banks[b0][:, :], gup_ap, sgd, True, True, [(tag, "lor"), gkey], BK(b0)); yield
        v_tt(dst, t1, banks[b0][:, :], ALU.mult, [k1] + BK(b0), [kdst]); yield

    BMx = lambda d: 2 + d

    def load_wtile(wq, src_cols, key):
        dma("pool", wq, src_cols, [], [key])

    def project3(*a_):
        for _ in project3_g(*a_):
            pass

    def project3_g(wq, wkey, hsrc_fn, hkeys, segs, raw, rawkey):
        for k in range(3):
            for (c0, ncol, bk) in segs:
                for kc in range(8):
                    mm(banks[bk][:, 0:ncol], wq[:, kc, k * 128:(k + 1) * 128], hsrc_fn(kc, c0, ncol),
                       kc == 0, kc == 7, [wkey] + hkeys, BK(bk))
                yield
                ev(raw[:, k, c0:c0 + ncol], banks[bk][:, 0:ncol], BK(bk), [rawkey])
                yield

    coef_p = sb("coef_p", [128, 27, 3], F32)
    coef_s = sb("coef_s", [128, 9, 5], F32)
    for (cf, mu, n, masks) in ((coef_p, V("mu_pp"), 27, ("mP", "mN")), (coef_s, V("mu_ss"), 9, ("mL", "mR", "mU", "mD"))):
        v_ts(cf[:, :, 0], mu, -1.0, 1.0, ALU.mult, ALU.add, ["vecs"], [("coef", n)])
        for i, m in enumerate(masks):
            v_ts(cf[:, :, i + 1], mu, C(m), None, ALU.mult, None, ["vecs", "cons"], [("coef", n)])

    ar.off = ar_scan0
    B = scan_buffers(ar)
    wl = ar.bf(3 * 1024).rearrange("p (c n) -> p c n", n=1024)
    dma("pool", wl[:, 0, :], wup_d, [], ["wl"])
    dma("pool", wl[:, 1, :], aup_d, [], ["wl"])
    dma("pool", wl[:, 2, :], gup_d, [], ["wl"])
    for s in range(2):
        for nm in ("Kt", "Bt"):
            P.op("dve", lambda e, t=B["slot"][s][nm]: e.memset(t, 0.0), reads=[], writes=[("P", "s", s, nm, 0), ("P", "s", s, nm, 1)])
        P.op("dve", lambda e, t=B["slot"][s]["KR"]: e.memset(t, 0.0), reads=[],
             writes=[("P", "s", s, nm_, h_) for nm_ in ("Kp", "Rt") for h_ in range(2)])
    P.op("dve", lambda e: e.memset(B["vbd"], 0.0), reads=[], writes=[("P", "vbd", 0), ("P", "vbd", 1)])
    P.barrier()

    wpp = w_mix_pp.rearrange("(kc p) n -> p kc n", p=128)
    lor = B["lor"]
    _off = ar.off
    rawL = [B["raw"], ar.bf(3 * 640).rearrange("p (c n) -> p c n", n=640)]
    mixL = [B["mixd"], ar.f32(3 * 512).rearrange("p (c n) -> p c n", n=512)]
    ar.off = _off
    MK = [{nm: ("P", nm, i_) for nm in ("rm", "km", "vm")} for i_ in range(2)]

    def h_prompt(kc, c0, ncol):
        return hT[:, kc, c0:c0 + ncol]

    HP = [("hT", c_, 0) for c_ in range(8)]
    load_wtile(B["wq"][0], wpp[:, :, 0:384], ("wq", 0))
    project3(B["wq"][0], ("wq", 0), h_prompt, HP, [(0, 512, 0)], rawL[0], ("raw", 0))
    shift_mix(rawL[0], 512, 0, coef_p, 3, lambda k: mixL[0][:, k, :], "seq", False, False, ("raw", 0),
              lambda k: MK[0][("rm", "km", "vm")[k]], 0)
    a_act(lor[:, 0, :], mixL[0][:, 0, :], AF.Tanh, [MK[0]["rm"]], [("P", "lor")])
    a_copy(lor[:, 1, :], mixL[0][:, 1, :], [MK[0]["km"]], [("P", "lor")])
    a_act(lor[:, 2, :], mixL[0][:, 2, :], AF.Sigmoid, [MK[0]["vm"]], [("P", "lor")])

    TfT = B["TfT"]
    Tfin = B["Tfin"]
    checkpoint("lora")

    def prompt_proj_g(pr):
        bi = (pr + 1) % 2
        wq = B["wq"][bi]
        load_wtile(wq, wpp[:, :, 384 * (pr + 1):384 * (pr + 2)], ("wq", bi))
        yield
        for _ in project3_g(wq, ("wq", bi), h_prompt, HP, [(0, 512, pr % 2)], rawL[bi], ("raw", bi)):
            yield
        for _ in shift_mix_g(rawL[bi], 512, 0, coef_p, 3, lambda k: mixL[bi][:, k, :], "seq", False, False,
                             ("raw", bi), lambda k: MK[bi][("rm", "km", "vm")[k]], 3 + 3 * pr):
            yield

    for _ in prompt_proj_g(0):
        pass
    for pr in range(8):
        bi = (pr + 1) % 2
        mixd = mixL[bi]
        tag = "P"
        nxt = [(lambda pr=pr: prompt_proj_g(pr + 1))] if pr < 7 else []
        vec = {"k_k": V("k_k", pr, pr + 1), "k_a": V("k_a", pr, pr + 1), "r_k": V("r_k", pr, pr + 1),
               "w0": [V("w0", d * 8 + pr, d * 8 + pr + 1) for d in range(2)],
               "a0": [V("a0", d * 8 + pr, d * 8 + pr + 1) for d in range(2)]}
        wts = {"wup": [wl[64 * d:64 * d + 64, 0, pr * 128:(pr + 1) * 128] for d in range(2)],
               "aup": [wl[64 * d:64 * d + 64, 1, pr * 128:(pr + 1) * 128] for d in range(2)]}

        def mk_ysink(d):
            S = B["slot"][d]

            def ysink(c, bank, bkey):
                for h in range(2):
                    dst = S["ydir"][64 * h:64 * h + 64, c * 64:(c + 1) * 64]
                    src = bank[64 * h:64 * h + 64, 384 + 64 * h:384 + 64 * h + 64]
                    if h == 0:
                        a_copy(dst, src, [bkey], [("P", "ydir", d, h)])
                    else:
                        v_copy(dst, src, [bkey], [("P", "ydir", d, h)])
            return ysink

        def mk_final(d, pr):
            def final(c, psrc, pkey, gam, gkey, by):
                seq = c // 4
                a_copy(Tfin, psrc, [pkey, gkey], ["Tfin"], scale=gam)
                P.op("pe", lambda e: e.transpose(banks[by][:, 384:512], Tfin, C("ident")),
                     reads=["Tfin", "cons"], writes=[("bank", by)])
                v_copy(TfT, banks[by][:, 384:512], [("bank", by)], ["TfT"])
                for h in range(2):
                    row0 = ((seq * 2 + d) * 16 + 2 * pr + h) * 64
                    dma("sp", st_out[row0:row0 + 64, :], TfT[64 * h:64 * h + 64, 64 * h:64 * h + 64], ["TfT"], [])
            return final

        scan_prep(B, tag, mixd[:, 0, :], mixd[:, 1, :], mixd[:, 2, :], lor, vec, wts, 0, 6, [(0, 0), (1, 1)],
                  mk=MK[bi], extra_chains=nxt)
        groups = []
        for d in range(2):
            for seq in range(2):
                chunks = list(range(4 * seq, 4 * seq + 4))
                if d == 1:
                    chunks = chunks[::-1]
                gi_ = 2 * d + seq
                groups.append({"dir": d, "slot": d, "tset": gi_, "vt": 0, "banks": (2 * gi_, 2 * gi_ + 1), "chunks": chunks,
                               "T0init": None, "restart": None, "final_at": {chunks[-1]}, "final": mk_final(d, pr),
                               "ysink": mk_ysink(d)})
        scan_chunks(B, tag, groups)
        ys = TP[8]
        v_tt(ys, B["slot"][0]["ydir"], B["slot"][1]["ydir"], ALU.add,
             [("P", "ydir", 0, 0), ("P", "ydir", 0, 1), ("P", "ydir", 1, 0), ("P", "ydir", 1, 1)], [TPK[8]], eng="pool")
        finalize_y(B, tag, ys, TPK[8], TPK[6], lor, wl[:, 2, pr * 128:(pr + 1) * 128],
                   V("lnx_g", pr, pr + 1), V("lnx_b", pr, pr + 1), yfin_p[:, pr, :], ("yfin", pr, 0))
        checkpoint("pair")

    def sample_mixer():
        P.barrier()
        xTf = xT[:, :, :].rearrange("p c n -> p (c n)")
        dma("sp", xsave, xTf, [("xT", c_, t_) for c_ in range(8) for t_ in range(3)], ["xsave"])
        P.barrier()
        xr = Arena(xTf, 8 * NT)

        def tset(a_):
            d = {}
            d["Am"] = [a_.bf(512) for _ in range(2)]
            d["PTm"] = a_.bf(128)
            d["SS"] = [a_.bf(256) for _ in range(2)]
            d["Nb"] = [a_.bf(128) for _ in range(2)]
            d["Xs"] = a_.bf(128)
            d["Un"] = a_.bf(128)
            d["T0"] = [a_.bf(128) for _ in range(2)]
            d["KB"] = [a_.bf(256) for _ in range(2)]
            d["Ktok"] = [kb_[:, 0:128] for kb_ in d["KB"]]
            d["Btok"] = [kb_[:, 128:256] for kb_ in d["KB"]]
            return d

        Bs = dict(B)
        Bs["slot"] = list(B["slot"])
        for s_ in range(2):
            d = {}
            for nm in ("Kt", "Bt"):
                d[nm] = xr.bf(8 * 128).rearrange("p (c n) -> p c n", n=128)
            d["KR"] = xr.bf(8 * 256).rearrange("p (c n) -> p c n", n=256)
            d["Kp"], d["Rt"] = d["KR"][:, :, 0:128], d["KR"][:, :, 128:256]
            d["ydir"] = xr.f32(512)
            d["gam"] = xr.f32(8)
            d["cse"] = xr.f32(16)
            Bs["slot"].append(d)
        Bs["vtok"] = list(B["vtok"]) + [xr.bf(8 * 128).rearrange("p (c n) -> p c n", n=128) for _ in range(2)]
        Bs["tset"] = [B["tset"][0], B["tset"][1], tset(xr), tset(xr)]
        lorv = [B["lor"], xr.bf(3 * 512).rearrange("p (c n) -> p c n", n=512)]
        TPs = list(TP) + [xr.f32(512), xr.f32(512)]
        TPKs = list(TPK) + [("tp", 10), ("tp", 11)]
        BON = [6, 8, 10, 11]
        hs = ar.bf(8 * 640).rearrange("p (k n) -> p k n", n=640)
        wq3 = [wl.rearrange("p c n -> p (c n)").rearrange("p (k n) -> p k n", n=384), B["wq"][0], B["wq"][1]]
        wls = ar.bf(3 * 256).rearrange("p (c n) -> p c n", n=256)
        tcv = TP[9][:, 256:512].bitcast(BF16)
        Tcar = [[tcv[:, (2 * pl_ + d_) * 128:(2 * pl_ + d_ + 1) * 128] for d_ in range(2)] for pl_ in range(2)]
        ybuf = TP[9].bitcast(BF16)[:, 0:512]
        s0tmp = TP[8][:, 0:64]
        wss = w_mix_ss.rearrange("(kc p) n -> p kc n", p=128)
        for i in range(3):
            dma("pool", wq3[i], wss[:, :, 384 * i:384 * (i + 1)], [], [("wq3", i)])
        dma("pool", wls[:, 0, :], wup_sd, [], ["wls"])
        dma("pool", wls[:, 1, :], aup_sd, [], ["wls"])
        dma("pool", wls[:, 2, :], gup_sd, [], ["wls"])
        P.op("dve", lambda e: e.memset(hs, 0.0), reads=[], writes=["hs"])
        for s_ in range(4):
            for nm in ("Kt", "Bt"):
                P.op("dve", lambda e, t=Bs["slot"][s_][nm]: e.memset(t, 0.0), reads=[],
                     writes=[("P", "s", s_, nm, 0), ("P", "s", s_, nm, 1)])
            P.op("dve", lambda e, t=Bs["slot"][s_]["KR"]: e.memset(t, 0.0), reads=[],
                 writes=[("P", "s", s_, nm_, h_) for nm_ in ("Kp", "Rt") for h_ in range(2)])
        P.op("dve", lambda e: e.memset(B["vbd"], 0.0), reads=[], writes=[("P", "vbd", 0), ("P", "vbd", 1)])
        for pl in range(2):
            for d in range(2):
                tc = Tcar[pl][d]
                P.op("pool", lambda e, tc=tc: e.memset(tc, 0.0), reads=[], writes=[("Tcar", pl, d)])
                row0 = (d * 4 + 2 * pl) * 64
                dma("sp", s0tmp, s0T_d[row0:row0 + 128, :], [], [TPK[8]])
                for h in range(2):
                    v_copy(tc[64 * h:64 * h + 64, 64 * h:64 * h + 64], s0tmp[64 * h:64 * h + 64, :], [TPK[8]],
                           [("Tcar", pl, d)], eng="pool")
        agv = [ag1_out[hf].rearrange("(r c p) t -> p r c t", p=128, c=4) for hf in range(2)]
        tag = "P"
        yp_ar = Arena()
        rawL = [B["raw"], xr.bf(3 * 640).rearrange("p (c n) -> p c n", n=640)]
        mixL = [B["mixd"], yp_ar.f32(3 * 512).rearrange("p (c n) -> p c n", n=512)]
        MKs = [{nm: ("P", nm, i_) for nm in ("rm", "km", "vm")} for i_ in range(2)]

        def h_seg(kc, c0, ncol):
            return hs[:, kc, c0:c0 + ncol]

        def sample_proj_g(i, d, pl, bi):
            s = i if d == 0 else 7 - i
            lor = lorv[d]
            raw, mixd, mk = rawL[bi], mixL[bi], MKs[bi]
            t0, t1 = max(512 * s - 64, 0), min(512 * s + 576, 4096)
            w0 = 512 * s - 64
            c_lo, c_hi = t0 - w0, t1 - w0
            pieces = [(c_lo, min(512, c_hi - c_lo), 0)]
            if c_hi - c_lo > 512:
                pieces.append((c_lo + 512, c_hi - c_lo - 512, 1))
            first, last = (s == 0), (s == 7)
            mkf = lambda k: mk[("rm", "km", "vm")[k]]
            if pl == 0:
                tt_ = t0
                while tt_ < t1:
                    r = tt_ // 1024
                    te = min(t1, (r + 1) * 1024)
                    for hf in range(2):
                        dma("sp", hs[:, 4 * hf:4 * hf + 4, tt_ - w0:te - w0], agv[hf][:, r, :, tt_ - r * 1024:te - r * 1024],
                            [("ag1", hf)], ["hs"])
                    tt_ = te
                yield
                for _ in project3_g(wq3[0], ("wq3", 0), h_seg, ["hs"], pieces, raw, ("raw", bi)):
                    yield
                for _ in shift_mix_g(raw, 512, 64, coef_s, 3, lambda k: mixd[:, k, :], "grid", first, last, ("raw", bi),
                                     mkf, 0):
                    yield
                a_act(lor[:, 0, :], mixd[:, 0, :], AF.Tanh, [mk["rm"]], [(tag, "lor")])
                a_copy(lor[:, 1, :], mixd[:, 1, :], [mk["km"]], [(tag, "lor")])
                a_act(lor[:, 2, :], mixd[:, 2, :], AF.Sigmoid, [mk["vm"]], [(tag, "lor")])
                yield
            for _ in project3_g(wq3[1 + pl], ("wq3", 1 + pl), h_seg, ["hs"], pieces, raw, ("raw", bi)):
                yield
            for _ in shift_mix_g(raw, 512, 64, coef_s, 3, lambda k: mixd[:, k, :], "grid", first, last, ("raw", bi),
                                 mkf, 3 + 3 * pl):
                yield

        pre_done = [False]
        for i in range(8):
            groups = []
            fin = []
            combos = [(0, 0), (0, 1), (1, 0), (1, 1)]
            if not pre_done[0]:
                for _ in sample_proj_g(i, 0, 0, 0):
                    pass
            pre_done[0] = False
            for ci, (d, pl) in enumerate(combos):
                bi = ci % 2
                mixd, mk = mixL[bi], MKs[bi]
                s = i if d == 0 else 7 - i
                lor = lorv[d]
                if True:
                    sl = 2 * d + pl
                    vec = {"k_k": V("k_k_s", pl, pl + 1), "k_a": V("k_a_s", pl, pl + 1), "r_k": V("r_k_s", pl, pl + 1),
                           "w0": [V("w0_s", dd * 2 + pl, dd * 2 + pl + 1) for dd in range(2)],
                           "a0": [V("a0_s", dd * 2 + pl, dd * 2 + pl + 1) for dd in range(2)]}
                    wts = {"wup": [wls[64 * dd:64 * dd + 64, 0, pl * 128:(pl + 1) * 128] for dd in range(2)],
                           "aup": [wls[64 * dd:64 * dd + 64, 1, pl * 128:(pl + 1) * 128] for dd in range(2)]}
                    tc = Tcar[pl][d]
                    ys = Bs["slot"][sl]["ydir"]
                    second = (i >= 4)

                    def ysink(c, bank, bkey, pl=pl, s=s, sl=sl, ys=ys, second=second):
                        for h in range(2):
                            src = bank[64 * h:64 * h + 64, 384 + 64 * h:384 + 64 * h + 64]
                            yf = yfwd[64 * h:64 * h + 64, pl, s * 512 + c * 64:s * 512 + (c + 1) * 64]
                            if not second:
                                if h == 0:
                                    a_copy(yf, src, [bkey], [("yst", pl, s, h)])
                                else:
                                    v_copy(yf, src, [bkey], [("yst", pl, s, h)])
                            else:
                                v_tt(ys[64 * h:64 * h + 64, c * 64:(c + 1) * 64], src, yf, ALU.add,
                                     [bkey, ("yst", pl, s, h)], [("ysum", sl, h)])

                    def t0init(T0, key, tc=tc, pl=pl, d=d):
                        v_copy(T0, tc, [("Tcar", pl, d)], [key], eng="pool")

                    def final(c, psrc, pkey, gam, gkey, by, tc=tc, pl=pl, d=d):
                        a_copy(tc, psrc, [pkey, gkey], [("Tcar", pl, d)], scale=gam)

                    chunks = list(range(8)) if d == 0 else list(range(7, -1, -1))
                    nxt = []
                    if ci < 3:
                        d2, pl2 = combos[ci + 1]
                        nxt = [(lambda d2=d2, pl2=pl2, ci=ci: sample_proj_g(i, d2, pl2, (ci + 1) % 2))]
                    elif i < 4:
                        nxt = [(lambda: sample_proj_g(i + 1, 0, 0, 0))]
                        pre_done[0] = True
                    scan_prep(Bs, tag, mixd[:, 0, :], mixd[:, 1, :], mixd[:, 2, :], lor, vec, wts, sl, BON[sl],
                              [(d, sl)], TPs, TPKs, mk=mk, extra_chains=nxt)
                    groups.append({"dir": d, "slot": sl, "tset": sl, "vt": sl, "banks": (2 * sl, 2 * sl + 1),
                                   "chunks": chunks, "T0init": t0init, "restart": None, "final_at": {chunks[-1]},
                                   "final": final, "ysink": ysink})
                    if second:
                        fin.append((d, pl, sl, s, lor))
            scan_chunks(Bs, tag, groups)
            ybufs = [ybuf, TPs[7].bitcast(BF16)[:, 0:512]]
            for f0 in range(0, len(fin), 2):
                chains = []
                for fi, (d, pl, sl, s, lor) in enumerate(fin[f0:f0 + 2]):
                    def one(d=d, pl=pl, sl=sl, s=s, lor=lor, fi=fi):
                        yb = ybufs[fi]
                        for _ in finalize_y_g(Bs, tag, Bs["slot"][sl]["ydir"], None, TPKs[BON[sl]], lor,
                                              wls[:, 2, pl * 128:(pl + 1) * 128], V("lnx_g_s", pl, pl + 1),
                                              V("lnx_b_s", pl, pl + 1), yb, (("ybuf", 0), TPKs[7])[fi],
                                              ykeys=[("ysum", sl, 0), ("ysum", sl, 1)], gkey="wls", bon_ap=TPs[BON[sl]],
                                              tmp=((0, 1, 2), (3, 4, 5))[fi], bk=((4, 5), (6, 7))[fi], TPx=TPs, TPKx=TPKs):
                            yield
                        dma("sp", ag2_in[pl][:, s * 512:(s + 1) * 512], yb, [(("ybuf", 0), TPKs[7])[fi]], [("ag2in", pl)])
                        yield
                    chains.append(one)
                carry = []
                if f0 == 2 and i < 7:
                    carry = [sample_proj_g(i + 1, 0, 0, 0)]
                    pre_done[0] = True
                carry = run_chains(chains, carry)
                for g_ in carry:
                    for _ in g_:
                        pass
        P.barrier()
        dma("sp", xTf, xsave, ["xsave"], [("xT", c_, t_) for c_ in range(8) for t_ in range(3)])

    def merge_phase(tiles, before_ya=None):
        P.barrier()
        am = Arena()
        am.off = ar_scan0
        worw = am.bf(8 * 1024).rearrange("p (c n) -> p c n", n=1024)
        wout = worw
        wopl = am.bf(4 * 1024).rearrange("p (c n) -> p c n", n=1024)
        wpl = am.bf(4 * 128).rearrange("p (c n) -> p c n", n=128)
        wg = [am.bf(8 * 256).rearrange("p (k n) -> p k n", n=256) for _ in range(2)]
        gates = am.bf(16 * 512).rearrange("p (c n) -> p c n", n=512)
        merged = am.bf(8 * 512).rearrange("p (c n) -> p c n", n=512)
        ub = am.bf(4 * 512).rearrange("p (c n) -> p c n", n=512)
        pinA = am.f32(4 * 544).rearrange("p (c n) -> p c n", n=544)
        pW = [am.f32(544) for _ in range(2)]
        hal = am.bf(8 * 16).rearrange("p (k n) -> p k n", n=16)
        halall = am.bf(4 * 8 * 16).rearrange("p (r k n) -> p r k n", k=8, n=16)
        if any(t_[0] > 0 for t_ in tiles):
            for hf in range(2):
                agh = ag1_out[hf].rearrange("(r c p) t -> p r c t", p=128, c=4)
                for r_ in range(4):
                    dma("sp", halall[:, r_, 4 * hf:4 * hf + 4, 0:8], agh[:, r_, :, 1016:1024], [("ag1", hf)], ["halall"])
                    dma("sp", halall[:, r_, 4 * hf:4 * hf + 4, 8:16], agh[:, r_, :, 0:8], [("ag1", hf)], ["halall"])

        def pick_halo(dst, lo, oh):
            v_ts(dst, halall[:, 0, :, lo:lo + 8], oh[:, 0:1], None, ALU.mult, None, ["halall", "vecs"], ["hal"])
            for r_ in range(1, 4):
                v_stt(dst, halall[:, r_, :, lo:lo + 8], oh[:, r_:r_ + 1], dst, ALU.mult, ALU.add, ["halall", "vecs", "hal"], ["hal"])
        dma("pool", wopl, wopool_d.rearrange("(c p) n -> p c n", p=128), [], ["wopl"])
        dma("pool", wpl, wpool_d.rearrange("(c p) n -> p c n", p=128), [], ["wpl"])
        wpg = w_mix_pg.rearrange("(kc p) n -> p kc n", p=128)
        for (t, a, b, g) in tiles:
            HK = [("hT", c_, g) for c_ in range(8)]
            nseq_, Ts_ = (2, 256) if t == 0 else (1, 512)
            Lp_ = Ts_ + 16
            pAv = lambda gi_: pinA[:, gi_, 0:nseq_ * Lp_].rearrange("p (s l) -> p s l", l=Lp_)
            pAall = pinA[:, :, 0:nseq_ * Lp_].rearrange("p c (s l) -> p c s l", l=Lp_)
            P.op("pool", lambda e: e.memset(pinA[:, :, :], 0.0), reads=[], writes=["pinA"])
            if t > 0:
                if t == 1:
                    pick_halo(hal[:, :, 0:8], 0, V("ohL"))
                    v_copy(hal[:, :, 8:16], hT[:, :, b:b + 8], [("hT", c_, 1) for c_ in range(8)], ["hal"], eng="pool")
                else:
                    v_copy(hal[:, :, 0:8], hT[:, :, a - 8:a], [("hT", c_, 1) for c_ in range(8)], ["hal"], eng="pool")
                    pick_halo(hal[:, :, 8:16], 8, V("ohR"))
            for ti in range(10):
                w_ = wg[ti % 2]
                dma("pool", w_, wpg[:, :, ti * 256:(ti + 1) * 256], [], [("wg", ti % 2)])
                for j2 in range(2):
                    ch = ti * 2 + j2
                    bk = 4 + ch % 2
                    for kc in range(8):
                        mm(banks[bk][:, :], w_[:, kc, j2 * 128:(j2 + 1) * 128], hT[:, kc, a:b], kc == 0, kc == 7,
                           [("wg", ti % 2)] + HK, BK(bk))
                    if ch < 4:
                        ev(pAv(ch)[:, :, 8:8 + Ts_], banks[bk][:, :].rearrange("p (s l) -> p s l", l=Ts_), BK(bk), ["pinA"])
                        if t > 0:
                            for kc in range(8):
                                mm(banks[6][:, 0:16], w_[:, kc, j2 * 128:(j2 + 1) * 128], hal[:, kc, :], kc == 0, kc == 7,
                                   [("wg", ti % 2), "hal"], BK(6))
                            a_copy(pinA[:, ch, 0:8], banks[6][:, 0:8], BK(6), ["pinA"])
                            a_copy(pinA[:, ch, 520:528], banks[6][:, 8:16], BK(6), ["pinA"])
                    else:
                        a_act(gates[:, ch - 4, :], banks[bk][:, :], AF.Sigmoid, BK(bk), [("gates", ch - 4)])
            nseq, Ts = (2, 256) if t == 0 else (1, 512)
            Lp = Ts + 16
            if t == 0:
                sclL, sclR = C("psclL_p"), C("psclR_p")
            else:
                sclL, sclR = V("psclL_s%d" % t), V("psclR_s%d" % t)
            for gi in range(4):
                w = POOL_W[gi]
                pv = lambda buf: buf[:, 0:nseq * Lp].rearrange("p (s l) -> p s l", l=Lp)
                src = pAv(gi)
                ks = "pinA"
                work = [(pv(pW[0]), "pw0"), (pv(pW[1]), "pw1")]
                sh = 1
                cur = Lp
                for lvl in range(gi + 1):
                    cur -= sh
                    dst, kd = work[lvl % 2]
                    v_tt(dst[:, :, 0:cur], src[:, :, 0:cur], src[:, :, sh:sh + cur], ALU.add, [ks], [kd],
                         eng=("pool" if lvl % 2 else "dve"))
                    src, ks = dst, kd
                    sh *= 2
                res, kres = work[(gi + 1) % 2]
                off = 8 - w // 2
                v_ts(res[:, :, 8:8 + Ts], src[:, :, off:off + Ts], 1.0 / w, None, ALU.mult, None, [ks], [kres])
                v_tt(res[:, :, 8:16], src[:, :, off:off + 8],
                     sclL[:, gi * 8:gi * 8 + 8].unsqueeze(1).to_broadcast([128, nseq, 8]), ALU.mult,
                     [ks, kres, "vecs", "cons"], [kres], eng="pool")
                v_tt(res[:, :, Ts:Ts + 8], src[:, :, off + Ts - 8:off + Ts],
                     sclR[:, gi * 8:gi * 8 + 8].unsqueeze(1).to_broadcast([128, nseq, 8]), ALU.mult,
                     [ks, kres, "vecs", "cons"], [kres], eng="pool")
                v_tt(ub[:, gi, :].rearrange("p (s l) -> p s l", l=Ts), res[:, :, 8:8 + Ts], pAv(gi)[:, :, 8:8 + Ts],
                     ALU.subtract, [kres, "pinA"], [("ubin", gi)])
            for gi in range(4):
                bk = 4 + gi % 2
                mm(banks[bk][:, :], wpl[:, gi, :], ub[:, gi, :], True, True, ["wpl", ("ubin", gi)], BK(bk))
                a_copy(ub[:, gi, :], banks[bk][:, :], BK(bk), [("ub", gi)], scale=V("pool_scale", gi, gi + 1))
            yf = yfin_p if t == 0 else yfin_s[:, :, (t - 1) * 512:t * 512]
            if before_ya is not None:
                before_ya()
                before_ya = None
            dma("pool", worw, worw_d.rearrange("(c p) n -> p c n", p=128), [], ["worw"])
            for oc in range(8):
                for pr in range(8):
                    mm(banks[4][:, :], worw[:, pr, oc * 128:(oc + 1) * 128], yf[:, pr, :], pr == 0, pr == 7,
                       ["worw", ("yfin", pr, t)], BK(4))
                for gi in range(4):
                    mm(banks[5][:, :], wopl[:, gi, oc * 128:(oc + 1) * 128], ub[:, gi, :], gi == 0, gi == 3,
                       ["wopl", ("ub", gi)], BK(5))
                tm = TP[oc % 2]
                v_tt(tm, gates[:, oc, :], banks[4][:, :], ALU.mult, [("gates", oc)] + BK(4), [TPK[oc % 2]])
                tm2 = TP[2 + oc % 2]
                v_tt(tm2, gates[:, 8 + oc, :], banks[5][:, :], ALU.mult, [("gates", 8 + oc)] + BK(5), [TPK[2 + oc % 2]])
                v_tt(merged[:, oc, :], tm, tm2, ALU.add, [TPK[oc % 2], TPK[2 + oc % 2]], [("merged", oc)], eng="pool")
            dma("pool", wout, wout_d.rearrange("(c p) n -> p c n", p=128), [], ["worw"])
            for oc2 in range(8):
                bk = 6 + oc2 % 2
                for oc in range(8):
                    mm(banks[bk][:, :], wout[:, oc, oc2 * 128:(oc2 + 1) * 128], merged[:, oc, :], oc == 0, oc == 7,
                       ["worw", ("merged", oc)], BK(bk))
                v_stt(xT[:, oc2, a:b], banks[bk][:, :], mod[:, 5, oc2, g:g + 1], xT[:, oc2, a:b], ALU.mult, ALU.add,
                      BK(bk) + [("xT", oc2, t)], [("xT", oc2, t)])

    if stage == 2:
        merge_phase([TT[0]])
        P.barrier()
        layer_norm(ln_final(1))
        P.muted = False
        P.barrier()
        write_out(y_out, 0, NT)
        return finish()

    merge_phase([TT[0]])
    sample_mixer()
    for pl in range(2):
        P.op("pool", lambda e, pl=pl: e.collective_compute("AllGather", ALU.bypass, replica_groups=RG,
                                                           ins=[ag2_in[pl].opt()], outs=[ag2_out[pl].opt()]),
             reads=[("ag2in", pl)], writes=[("ag2", pl)], kind="cc")

    def load_yfin(e, pl):
        r = e.alloc_register("qown%d" % pl)
        e.reg_load(r, qidx_d[0:1, 2:3])
        qv = e.snap(r, min_val=0, max_val=3)
        src = ag2_out[pl].rearrange("(r p) t -> p r t", p=128)[:, :, bass.ds(qv * 1024, 1024)]
        dst = yfin_s.rearrange("p (r two) n -> p r two n", two=2)[:, :, pl, :]
        return e.dma_start(out=dst, in_=src)

    def fetch_yfin():
        for pl in range(2):
            P.op("pool", lambda e, pl=pl: load_yfin(e, pl), reads=[("ag2", pl)],
                 writes=[("yfin", pr_, t_) for pr_ in range(8) for t_ in (1, 2)], kind="d")

    merge_phase(TT[1:], before_ya=fetch_yfin)
    P.barrier()
    if stage == 3:
        layer_norm(ln_final(1))
        P.muted = False
        P.barrier()
        write_out(y_out, 0, NT)
        return finish()
    layer_norm(ln_outs(1, 5, 6))
    P.barrier()
    ffn(1, 7)
    layer_norm(ln_final(2))
    P.muted = False
    P.barrier()
    write_out(y_out, 0, NT)
    return finish()


_CACHE = {}


def _host_inputs(inputs):
    g = lambda k: np.asarray(inputs[k][0], np.float32)
    w_mix = g("w_mix_in")
    common = {
        "w_mod": g("w_mod"), "ffn_in": g("ffn_in"), "ffn_out": g("ffn_out"),
        "w_mix_pp": np.ascontiguousarray(w_mix[:, _mix_cols(range(8))]),
        "w_mix_pg": np.ascontiguousarray(w_mix[:, 3456:]),
        "wup": np.ascontiguousarray(g("w_up").reshape(128, D)),
        "aup": np.ascontiguousarray(g("a_up").reshape(128, D)),
        "gup": g("g_up"),
        "w_o_rwkv": g("w_o_rwkv"), "w_pool": np.ascontiguousarray(g("w_pool").reshape(512, 128)),
        "w_o_pool": g("w_o_pool"), "w_out": g("w_out"),
    }
    cp = _const_pack()
    cons = cp.build()
    in_maps = []
    vp0 = None
    for core in range(8):
        b, q = core // 4, core % 4
        vp = _vec_layout(inputs, core)
        if vp0 is None:
            vp0 = vp
        xin = np.concatenate([np.asarray(inputs["x_prompt"])[2 * core:2 * core + 2].reshape(NP_, D),
                              np.asarray(inputs["x_sample"])[b, q * NS_:(q + 1) * NS_]], axis=0)
        cvec = np.stack([_fm(inputs["c_ctx"]), _fm(np.asarray(inputs["c"])[b])], axis=2).reshape(128, 16)
        st = np.asarray(inputs["state_rwkv"], np.float32)[b, 0]
        s0T = np.ascontiguousarray(st[:, 4 * q:4 * q + 4].transpose(0, 1, 3, 2)).reshape(512, 64)
        m = {
            "xin": np.ascontiguousarray(xin, np.float32),
            "cvec": np.ascontiguousarray(cvec, np.float32),
            "vecs": vp.build(), "cons": cons, "cmask": cp.masks,
            "w_mix_ss": np.ascontiguousarray(w_mix[:, _mix_cols([2 * q, 2 * q + 1])]),
            "wup_s": np.ascontiguousarray(common["wup"][:, 256 * q:256 * q + 256]),
            "aup_s": np.ascontiguousarray(common["aup"][:, 256 * q:256 * q + 256]),
            "gup_s": np.ascontiguousarray(common["gup"][:, 256 * q:256 * q + 256]),
            "s0T": s0T,
            "qidx": np.array([[max(q - 1, 0), min(q + 1, 3), q, 0]], np.int32),
        }
        m.update(common)
        in_maps.append(m)
    return in_maps, vp0, cp


def kernel(**inputs):
    in_maps, vp0, cp = _host_inputs(inputs)
    if "prog" not in _CACHE:
        _CACHE["prog"] = build_program(vp0.cols, cp.cols, vp0.n, cp.n)
    nc = _CACHE["prog"]
    res = run_bass_kernel_spmd(nc, in_maps, core_ids=list(range(8)))
    outs = [r["y_out"] for r in res.results]
    y_p = np.stack([o[:NP_] for o in outs]).reshape(16, 256, D).astype(np.float32)
    y_s = np.stack([o[NP_:] for o in outs]).reshape(2, 4096, D).astype(np.float32)
    st = np.stack([r["st_out"].reshape(2, 2, 16, 64, 64) for r in res.results]).reshape(16, 1, 2, 16, 64, 64)
    return y_p, y_s, st.astype(np.float32)
```

```python
from contextlib import ExitStack
import numpy as np
import concourse.bass as bass
import concourse.mybir as mybir
from concourse.bass_utils import run_bass_kernel_spmd

F32 = mybir.dt.float32
BF16 = mybir.dt.bfloat16
I32 = mybir.dt.int32
AF = mybir.ActivationFunctionType
ALU = mybir.AluOpType

D = 1024
NP_ = 512
NS_ = 1024
NT = NP_ + NS_
DFF = 2816
ALPHA = 2.0 ** 0.25
LN_EPS = 1e-5
GN_EPS = 64e-5
STAGE = 4


class _Op:
    __slots__ = ("eng", "fn", "waits", "kind", "sem", "val")


class Prog:
    ENG = ("pe", "act", "dve", "pool", "sp")
    K = 6

    def __init__(self, nc):
        self.nc = nc
        self.ops = {e: [] for e in self.ENG}
        self.count = {e: 0 for e in self.ENG}
        self.known = {e: {} for e in self.ENG}
        self.last_w = {}
        self.rd_eng = {}
        self.rd_dma = {}
        self.dma_n = {e: 0 for e in self.ENG}
        self.ncc = 0
        self.dma_final = {}
        self.muted = False

    def _need(self, eng, d, waits):
        if d is None:
            return
        if d.kind == "c" and d.eng == eng and eng == "pe":
            return
        if self.known[eng].get(d.sem, 0) >= d.val:
            return
        if waits.get(d.sem, 0) < d.val:
            waits[d.sem] = d.val

    def op(self, eng, fn, reads=(), writes=(), kind="c"):
        if self.muted:
            return None
        r2, w2 = [], []
        for k in reads:
            if isinstance(k, tuple) and k[0] in ("bank", "bankc"):
                w2.append(("bank", k[1]))
            else:
                r2.append(k)
        for k in writes:
            if isinstance(k, tuple) and k[0] in ("bank", "bankc"):
                k = ("bank", k[1])
            if k not in w2:
                w2.append(k)
        reads, writes = r2, w2
        o = _Op()
        o.eng = eng
        o.fn = fn
        o.kind = kind
        waits = {}
        for k in reads:
            self._need(eng, self.last_w.get(k), waits)
        for k in writes:
            self._need(eng, self.last_w.get(k), waits)
            for d in self.rd_eng.get(k, {}).values():
                self._need(eng, d, waits)
            for d in self.rd_dma.get(k, ()):
                self._need(eng, d, waits)
        if kind == "d":
            i = self.dma_n[eng]
            o.sem = ("dma", eng, i % self.K)
            o.val = 16 * (i // self.K + 1)
            if i >= self.K and self.known[eng].get(o.sem, 0) < o.val - 16:
                waits[o.sem] = max(waits.get(o.sem, 0), o.val - 16)
            self.dma_n[eng] += 1
            self.dma_final[o.sem] = o.val
        elif kind == "cc":
            o.sem = ("cc", self.ncc)
            o.val = 1
            self.ncc += 1
        else:
            self.count[eng] += 1
            o.sem = ("eng", eng)
            o.val = self.count[eng]
        for k, v in waits.items():
            self.known[eng][k] = v
        o.waits = list(waits.items())
        for k in reads:
            if kind == "d":
                self.rd_dma.setdefault(k, []).append(o)
            else:
                self.rd_eng.setdefault(k, {})[eng] = o
        for k in writes:
            self.last_w[k] = o
            self.rd_eng[k] = {}
            self.rd_dma[k] = []
        self.ops[eng].append(o)
        return o

    def barrier(self):
        if self.muted:
            return
        targets = {}
        for e in self.ENG:
            if self.count[e]:
                targets[("eng", e)] = self.count[e]
        targets.update(self.dma_final)
        for e in self.ENG:
            o = _Op()
            o.eng = e
            o.fn = None
            o.kind = "w"
            w = {}
            for k, v in targets.items():
                if k == ("eng", e):
                    continue
                if self.known[e].get(k, 0) < v:
                    w[k] = v
                    self.known[e][k] = v
            o.waits = list(w.items())
            self.ops[e].append(o)

    def emit(self):
        nc = self.nc
        with ExitStack() as es:
            sems = {}
            for e in self.ENG:
                sems[("eng", e)] = es.enter_context(nc.semaphore("s_" + e))
                for i in range(min(self.K, self.dma_n[e])):
                    sems[("dma", e, i)] = es.enter_context(nc.semaphore("d_%s%d" % (e, i)))
            for i in range(self.ncc):
                sems[("cc", i)] = es.enter_context(nc.semaphore("cc%d" % i))
            self.barrier()
            block = es.enter_context(nc.Block())

            def run(ename):
                def body(e):
                    for o in self.ops[ename]:
                        for k, v in o.waits:
                            e.wait_ge(sems[k], v)
                        if o.fn is None:
                            continue
                        ins = o.fn(e)
                        if o.kind == "d":
                            ins.then_inc(sems[o.sem], 16)
                        elif o.kind == "cc":
                            ins.then_inc(sems[o.sem])
                        else:
                            ins.then_inc(sems[o.sem], 1)
                return body

            block.tensor(run("pe"))
            block.scalar(run("act"))
            block.vector(run("dve"))
            block.gpsimd(run("pool"))
            block.sync(run("sp"))


def _fm(v):
    v = np.asarray(v, np.float32).reshape(-1, 128)
    return np.ascontiguousarray(v.T)


class VecPack:
    def __init__(self):
        self.cols = {}
        self.n = 0
        self.parts = []

    def add(self, name, arr):
        arr = np.asarray(arr, np.float32)
        assert arr.shape[0] == 128
        arr = arr.reshape(128, -1)
        self.cols[name] = (self.n, arr.shape[1])
        self.n += arr.shape[1]
        self.parts.append(arr)

    def build(self):
        return np.ascontiguousarray(np.concatenate(self.parts, axis=1))


POOL_W = (2, 4, 8, 16)


def _pool_scl(T, left_edge, right_edge):
    L = np.zeros((4, 8), np.float32)
    R = np.zeros((4, 8), np.float32)
    for gi, w in enumerate(POOL_W):
        for i in range(8):
            t = i
            cnt = (min(t + w - w // 2, T) - max(t - w // 2, 0)) if left_edge else w
            L[gi, i] = 1.0 / cnt
            t = T - 8 + i
            cnt = (min(t + w - w // 2, T) - max(t - w // 2, 0)) if right_edge else w
            R[gi, i] = 1.0 / cnt
    return L, R


def _const_pack():
    cp = VecPack()
    cp.add("ident", np.eye(128, dtype=np.float32))
    cp.add("ones_ln", np.full((128, 128), 1.0 / 1024.0, np.float32))
    blk = np.kron(np.eye(2, dtype=np.float32), np.ones((64, 64), np.float32))
    cp.add("blk1", blk)
    cp.add("blk64", blk / 64.0)
    s = np.arange(64)
    strict_f = (s[:, None] < s[None, :]).astype(np.float32)
    incl_f = (s[:, None] <= s[None, :]).astype(np.float32)
    t2 = lambda m: np.tile(m, (2, 2))
    cp.masks = np.concatenate([np.concatenate([t2(strict_f), t2(incl_f), t2(strict_f), t2(incl_f)], axis=1),
                               np.concatenate([t2(strict_f.T), t2(incl_f.T), t2(strict_f.T), t2(incl_f.T)], axis=1),
                               t2(strict_f.T), t2(strict_f)], axis=1).astype(np.float32)
    p = np.arange(128)
    for name, m, r in (("mP", 2, 0), ("mN", 2, 1), ("mL", 4, 0), ("mR", 4, 1), ("mU", 4, 2), ("mD", 4, 3)):
        cp.add(name, (p % m == r).astype(np.float32)[:, None])
    L, R = _pool_scl(256, True, True)
    cp.add("psclL_p", np.tile(L.reshape(1, 32), (128, 1)))
    cp.add("psclR_p", np.tile(R.reshape(1, 32), (128, 1)))
    return cp


def _mix_cols(pairs):
    cols = list(range(3072, 3456))
    for pr in pairs:
        for base in (0, 1024, 2048):
            cols += list(range(base + pr * 128, base + pr * 128 + 128))
    return np.array(cols)


def _vec_layout(inputs, core):
    b, q = core // 4, core % 4
    g = lambda k: np.asarray(inputs[k][0], np.float32)
    vp = VecPack()
    vp.add("b_mod", _fm(g("b_mod")))
    vp.add("ln_g", _fm(g("ln_g").reshape(-1)))
    vp.add("ln_b", _fm(g("ln_b").reshape(-1)))
    mu = g("mu_shift")
    vp.add("mu_pp", _fm(mu[_mix_cols(range(8))]))
    vp.add("mu_ss", _fm(mu[_mix_cols([2 * q, 2 * q + 1])]))
    for nm in ("k_k", "k_a", "lnx_g", "lnx_b"):
        v = _fm(g(nm))
        vp.add(nm, v)
        vp.add(nm + "_s", v[:, 2 * q:2 * q + 2])
    v = _fm(g("r_k").reshape(-1))
    vp.add("r_k", v)
    vp.add("r_k_s", v[:, 2 * q:2 * q + 2])
    for nm in ("w0", "a0"):
        v = np.stack([_fm(g(nm)[d]) for d in range(2)], axis=1)
        vp.add(nm, v.reshape(128, 16))
        vp.add(nm + "_s", np.ascontiguousarray(v[:, :, 2 * q:2 * q + 2]).reshape(128, 4))
    vp.add("pool_scale", _fm(g("pool_scale")))
    L, R = _pool_scl(4096, q == 0, False)
    vp.add("psclL_s1", np.tile(L.reshape(1, 32), (128, 1)))
    vp.add("psclR_s1", np.tile(R.reshape(1, 32), (128, 1)))
    L2, R2 = _pool_scl(4096, False, q == 3)
    vp.add("psclL_s2", np.tile(L2.reshape(1, 32), (128, 1)))
    vp.add("psclR_s2", np.tile(R2.reshape(1, 32), (128, 1)))
    ohL = np.zeros((128, 4), np.float32)
    ohR = np.zeros((128, 4), np.float32)
    if q > 0:
        ohL[:, q - 1] = 1.0
    if q < 3:
        ohR[:, q + 1] = 1.0
    vp.add("ohL", ohL)
    vp.add("ohR", ohR)
    return vp


DEBUG_STOP = None
LAM = 0.6065306597126334
AR_F32 = 26624


def build_program(vcols, ccols, nvec, ncon, stage=STAGE):
    nc = bass.Bass("TRN2", target_bir_lowering=False)
    P = Prog(nc)
    es = ExitStack()

    def dram_in(name, shape, dt=F32):
        return nc.dram_tensor(name, list(shape), dt, kind="ExternalInput").ap()

    def dram_out(name, shape, dt=F32):
        return nc.dram_tensor(name, list(shape), dt, kind="ExternalOutput").ap()

    def sb(name, shape, dt):
        return es.enter_context(nc.sbuf_tensor(name, list(shape), dt))

    xin = dram_in("xin", [NT, D])
    cvec = dram_in("cvec", [128, 16])
    vecs_d = dram_in("vecs", [128, nvec])
    cons_d = dram_in("cons", [128, ncon])
    cmask_d = dram_in("cmask", [128, 1280])
    w_mod = dram_in("w_mod", [D, 9 * D])
    ffn_in = dram_in("ffn_in", [2, D, 2 * DFF])
    ffn_out = dram_in("ffn_out", [2, DFF, D])
    w_mix_pp = dram_in("w_mix_pp", [D, 27 * 128])
    w_mix_ss = dram_in("w_mix_ss", [D, 9 * 128])
    w_mix_pg = dram_in("w_mix_pg", [D, 20 * 128])
    wup_d = dram_in("wup", [128, D])
    aup_d = dram_in("aup", [128, D])
    gup_d = dram_in("gup", [128, D])
    wup_sd = dram_in("wup_s", [128, 256])
    aup_sd = dram_in("aup_s", [128, 256])
    gup_sd = dram_in("gup_s", [128, 256])
    worw_d = dram_in("w_o_rwkv", [D, D])
    wpool_d = dram_in("w_pool", [512, 128])
    wopool_d = dram_in("w_o_pool", [512, D])
    wout_d = dram_in("w_out", [D, D])
    s0T_d = dram_in("s0T", [512, 64])
    qidx_d = dram_in("qidx", [1, 4], I32)
    y_out = dram_out("y_out", [NT, D])
    st_out = dram_out("st_out", [2 * 2 * 16 * 64, 64])
    xsave = nc.dram_tensor("xsave", [128, 8 * NT], F32).ap()
    ag1_in = [nc.dram_tensor("ag1_in%d" % i, [512, NS_], BF16).ap() for i in range(2)]
    ag1_out = [nc.dram_tensor("ag1_out%d" % i, [4 * 512, NS_], BF16).ap() for i in range(2)]
    ag2_in = [nc.dram_tensor("ag2_in%d" % i, [128, 4096], BF16).ap() for i in range(2)]
    ag2_out = [nc.dram_tensor("ag2_out%d" % i, [512, 4096], BF16).ap() for i in range(2)]
    RG = [[0, 1, 2, 3], [4, 5, 6, 7]]

    xT = sb("xT", [128, 8, NT], F32)
    hb = sb("hb", [128, 8 * NT], BF16)
    hT = hb[:, :].rearrange("p (c n) -> p c n", n=NT)
    woB = hb[:, 0:22 * 512].rearrange("p (j n) -> p j n", n=512)
    AR = sb("AR", [128, AR_F32], F32)
    vecs = sb("vecs_s", [128, nvec], F32)
    cons = sb("cons_s", [128, ncon], F32)
    cv = sb("cv", [128, 16], F32)
    cvs = sb("cvs", [128, 16], BF16)
    mod = sb("mod", [128, 9, 8, 2], F32)
    drv = sb("drv", [128, 12, 8, 2], F32)
    xtok = [sb("xtok%d" % i, [128, D], F32) for i in range(2)]
    tmpA = [sb("tmpA%d" % i, [128, 512], F32) for i in range(2)]
    tmpB = [sb("tmpB%d" % i, [128, 512], F32) for i in range(2)]
    lnm = sb("lnm", [128, 512], F32)
    lnr = sb("lnr", [128, 512], F32)
    identb = sb("identb", [128, 128], BF16)
    cmask = sb("cmask_s", [128, 1280], BF16)
    Dbuf = sb("Dbuf", [128, 2, 5, 128], BF16)
    MASKS = {"maskA_f": (0, 512), "maskA_b": (512, 1024), "maskPT_f": (1024, 1152), "maskPT_b": (1152, 1280)}

    def CM(name):
        a_, b_ = MASKS[name]
        return cmask[:, a_:b_]
    banks = [es.enter_context(nc.psum_tensor("bank%d" % i, [128, 512], F32)) for i in range(8)]

    class Arena:
        def __init__(self, base=None, size=None):
            self.off = 0
            self.base = AR if base is None else base
            self.size = AR_F32 if size is None else size

        def f32(self, n):
            a = self.base[:, self.off:self.off + n]
            self.off += n
            assert self.off <= self.size, self.off
            return a

        def bf(self, n):
            assert n % 2 == 0
            a = self.base[:, self.off:self.off + n // 2].bitcast(BF16)
            self.off += n // 2
            assert self.off <= self.size, self.off
            return a

    ar = Arena()
    act = ar.bf(22 * NT).rearrange("p (j n) -> p j n", n=NT)
    woA = ar.bf(22 * 512).rearrange("p (j n) -> p j n", n=512)
    win = [[ar.bf(8 * 256).rearrange("p (k n) -> p k n", n=256) for g in range(2)] for i in range(2)]
    actf = act.rearrange("p j n -> p (j n)")
    wm = [actf[:, i * 8192:(i + 1) * 8192].rearrange("p (k n) -> p k n", n=1024) for i in range(2)]

    def V(name, a=0, b=None):
        o, n = vcols[name]
        b = n if b is None else b
        return vecs[:, o + a:o + b]

    def C(name, a=0, b=None):
        o, n = ccols[name]
        b = n if b is None else b
        return cons[:, o + a:o + b]

    def mm(out, lhsT, rhs, start, stop, reads, writes):
        P.op("pe", lambda e: e.matmul(out, lhsT, rhs, start=start, stop=stop), reads=reads, writes=writes)

    def a_copy(dst, src, reads, writes, scale=None):
        if scale is None:
            P.op("act", lambda e: e.copy(dst, src), reads=reads, writes=writes)
        else:
            P.op("act", lambda e: e.activation(dst, src, AF.Identity, scale=scale), reads=reads, writes=writes)

    def a_act(dst, src, func, reads, writes, bias=None, scale=None):
        kw = {}
        if bias is not None:
            kw["bias"] = bias
        if scale is not None:
            kw["scale"] = scale
        P.op("act", lambda e: e.activation(dst, src, func, **kw), reads=reads, writes=writes)

    def v_copy(dst, src, reads, writes, eng="dve"):
        P.op(eng, lambda e: e.tensor_copy(dst, src), reads=reads, writes=writes)

    def v_tt(dst, a, b, op, reads, writes, eng="dve"):
        P.op(eng, lambda e: e.tensor_tensor(dst, a, b, op), reads=reads, writes=writes)

    def v_ts(dst, a, s1, s2, op0, op1, reads, writes, eng="dve"):
        if s2 is None:
            P.op(eng, lambda e: e.tensor_scalar(dst, a, s1, None, op0), reads=reads, writes=writes)
        else:
            P.op(eng, lambda e: e.tensor_scalar(dst, a, s1, s2, op0, op1), reads=reads, writes=writes)

    def v_stt(dst, a, s, b, op0, op1, reads, writes, eng="dve"):
        P.op("dve", lambda e: e.scalar_tensor_tensor(dst, a, s, b, op0, op1), reads=reads, writes=writes)

    def dma(eng, dst, src, reads, writes):
        P.op(eng, lambda e: e.dma_start(out=dst, in_=src), reads=reads, writes=writes, kind="d")

    evc = [0]

    def ev(dst, src, reads, writes):
        evc[0] += 1
        if evc[0] % 2:
            a_copy(dst, src, reads, writes)
        else:
            v_copy(dst, src, reads, writes)

    dma("sp", vecs[:, :], vecs_d, [], ["vecs"])
    dma("sp", cons[:, :], cons_d, [], ["cons"])
    dma("sp", cv[:, :], cvec, [], ["cv"])
    dma("pool", cmask[:, :], cmask_d, [], ["cons"])
    a_act(cvs[:, :], cv[:, :], AF.Silu, ["cv"], ["cvs"])
    v_copy(identb[:, :], C("ident"), ["cons"], ["identb"])

    for i in range(NT // 128):
        xt = xtok[i % 2]
        dma("sp", xt[:, :], xin[i * 128:(i + 1) * 128, :], [], [("xtok", i % 2)])
        for half in range(2):
            bk = banks[6 + half]
            for cc in range(4):
                c = half * 4 + cc
                P.op("pe", lambda e, bk=bk, cc=cc, xt=xt, c=c: e.transpose(
                    bk[:, cc * 128:(cc + 1) * 128], xt[:, c * 128:(c + 1) * 128], C("ident")),
                    reads=[("xtok", i % 2), "cons"], writes=[("bank", 6 + half)])
            dst = xT[:, half * 4:half * 4 + 4, i * 128:(i + 1) * 128]
            src = bk[:, :].rearrange("p (c n) -> p c n", n=128)
            wr = [("xT", c_, i // 4) for c_ in range(half * 4, half * 4 + 4)]
            if half == 0:
                a_copy(dst, src, [("bank", 6)], wr)
            else:
                v_copy(dst, src, [("bank", 7)], wr)

    wmv = w_mod.rearrange("(kc p) n -> p kc n", p=128)
    for i in range(9):
        wt = wm[i % 2]
        dma("pool", wt, wmv[:, :, i * 1024:(i + 1) * 1024], [], [("wm", i % 2)])
        bk = banks[4 + (i % 2)]
        for oc in range(8):
            for kc in range(8):
                mm(bk[:, oc * 2:oc * 2 + 2], wt[:, kc, oc * 128:(oc + 1) * 128], cvs[:, kc * 2:kc * 2 + 2],
                   kc == 0, kc == 7, [("wm", i % 2), "cvs"], [("bank", 4 + (i % 2))])
        v_tt(mod[:, i, :, :], bk[:, 0:16].rearrange("p (c g) -> p c g", g=2),
             V("b_mod", i * 8, i * 8 + 8).unsqueeze(2).to_broadcast([128, 8, 2]), ALU.add,
             [("bank", 4 + (i % 2)), "vecs"], [("mod", i)])

    def lng(l):
        return V("ln_g", l * 8, l * 8 + 8).unsqueeze(2).to_broadcast([128, 8, 2])

    def lnb(l):
        return V("ln_b", l * 8, l * 8 + 8).unsqueeze(2).to_broadcast([128, 8, 2])

    def small(fn, reads, writes):
        P.op("dve", fn, reads=reads, writes=writes)

    small(lambda e: e.tensor_scalar_add(drv[:, 0], mod[:, 1], 1.0), [("mod", 1)], [("drv", 0)])
    small(lambda e: e.tensor_scalar_mul(drv[:, 1], mod[:, 2], 0.5), [("mod", 2)], [("drv", 1)])
    small(lambda e: e.tensor_scalar_add(drv[:, 4], mod[:, 4], 1.0), [("mod", 4)], [("drv", 4)])
    small(lambda e: e.tensor_tensor(drv[:, 2], drv[:, 4], lng(0), ALU.mult), [("drv", 4), "vecs"], [("drv", 2)])
    small(lambda e: e.tensor_tensor(drv[:, 3], drv[:, 4], lnb(0), ALU.mult), [("drv", 4), "vecs"], [("drv", 3)])
    small(lambda e: e.tensor_tensor(drv[:, 3], drv[:, 3], mod[:, 3], ALU.add), [("drv", 3), ("mod", 3)], [("drv", 3)])
    small(lambda e: e.tensor_scalar_add(drv[:, 4], mod[:, 7], 1.0), [("mod", 7), ("drv", 2), ("drv", 3)], [("drv", 4)])
    small(lambda e: e.tensor_tensor(drv[:, 5], drv[:, 4], lng(1), ALU.mult), [("drv", 4), "vecs"], [("drv", 5)])
    small(lambda e: e.tensor_tensor(drv[:, 6], drv[:, 4], lnb(1), ALU.mult), [("drv", 4), "vecs"], [("drv", 6)])
    small(lambda e: e.tensor_tensor(drv[:, 6], drv[:, 6], mod[:, 6], ALU.add), [("drv", 6), ("mod", 6)], [("drv", 6)])
    small(lambda e: e.tensor_scalar_mul(drv[:, 7], mod[:, 8], 0.5), [("mod", 8)], [("drv", 7)])
    for l in range(2):
        small(lambda e, l=l: e.tensor_scalar_mul(drv[:, 8 + 2 * l], lng(l), ALPHA), ["vecs"], [("drv", 8 + 2 * l)])
        small(lambda e, l=l: e.tensor_scalar_mul(drv[:, 9 + 2 * l], lnb(l), ALPHA), ["vecs"], [("drv", 9 + 2 * l)])

    GRP = [(0, 0, NP_), (1, NP_, NT)]
    TT = [(t, t * 512, (t + 1) * 512, 0 if t == 0 else 1) for t in range(3)]
    ALLH = [("hT", c_, g_) for c_ in range(8) for g_ in range(2)]

    def xkeys(c, a, b):
        return [("xT", c, t) for t in range(a // 512, (b + 511) // 512)]

    def modulate(scale_idx, shift_idx):
        for c in range(8):
            for g, a, b in GRP:
                a_act(hT[:, c, a:b], xT[:, c, a:b], AF.Identity, xkeys(c, a, b), [("hT", c, g)],
                      bias=mod[:, shift_idx, c, g:g + 1], scale=drv[:, scale_idx, c, g:g + 1])
            v_ts(xT[:, c, :], xT[:, c, :], ALPHA, None, ALU.mult, None, xkeys(c, 0, NT), xkeys(c, 0, NT))

    def ffn(f, gate_idx):
        wi = ffn_in[f].rearrange("(kc p) n -> p kc n", p=128)
        wo = ffn_out[f].rearrange("(j p) n -> p j n", p=128)
        dma("pool", woA, wo[:, :, 0:512], [], ["woA"])
        nb = 0
        for jj in range(11):
            wb = win[jj % 2]
            for g2 in range(2):
                dma("pool", wb[g2], wi[:, :, g2 * DFF + jj * 256:g2 * DFF + (jj + 1) * 256], [],
                    [("win", jj % 2, g2)])
            for j2 in range(2):
                j = jj * 2 + j2
                for t, a, b, g in TT:
                    pg = (nb % 2) * 2
                    nb += 1
                    for g2 in range(2):
                        for kc in range(8):
                            mm(banks[pg + g2][:, :], wb[g2][:, kc, j2 * 128:(j2 + 1) * 128], hT[:, kc, a:b],
                               kc == 0, kc == 7, [("win", jj % 2, g2), ("hT", kc, g)], [("bank", pg + g2)])
                    tm = tmpA[nb % 2]
                    a_act(tm[:, :], banks[pg][:, :], AF.Silu, [("bank", pg)], [("tmpA", nb % 2)])
                    v_tt(act[:, j, a:b], tm[:, :], banks[pg + 1][:, :], ALU.mult,
                         [("tmpA", nb % 2), ("bank", pg + 1)], [("act", j, t)])
        dma("pool", woB, wo[:, :, 512:1024], [], ALLH)
        nb = 0
        for oc in range(8):
            wt = woA if oc < 4 else woB
            wkey = ["woA"] if oc < 4 else ALLH
            for t, a, b, g in TT:
                bk = 4 + nb % 2
                nb += 1
                for j in range(22):
                    mm(banks[bk][:, :], wt[:, j, (oc % 4) * 128:(oc % 4 + 1) * 128], act[:, j, a:b],
                       j == 0, j == 21, wkey + [("act", j, t)], [("bank", bk)])
                v_stt(xT[:, oc, a:b], banks[bk][:, :], drv[:, gate_idx, oc, g:g + 1], xT[:, oc, a:b],
                      ALU.mult, ALU.add, [("bank", bk), ("xT", oc, t)], [("xT", oc, t)])

    def run_chains(chains, carry=()):
        gens = [c() for c in chains]
        carry = list(carry)
        while gens:
            for g_ in list(gens):
                try:
                    next(g_)
                except StopIteration:
                    gens.remove(g_)
            for g_ in list(carry):
                try:
                    next(g_)
                except StopIteration:
                    carry.remove(g_)
        return carry

    def layer_norm(outs, tiles=TT):
        P.barrier()
        la = Arena()
        tmps = [{"m": la.f32(512), "r": la.f32(512), "q": [la.f32(512), la.f32(512)]} for _ in tiles]

        def tile_chain(ti, t, a, b, g):
            tm_, bm, br = tmps[ti], 2 * ti, 2 * ti + 1
            lnm_, lnr_ = tm_["m"], tm_["r"]
            km, kr = ("lnm", ti), ("lnr", ti)
            for c in range(8):
                mm(banks[bm][:, :], C("ones_ln"), xT[:, c, a:b], c == 0, c == 7, ["cons", ("xT", c, t)], [("bank", bm)])
            yield
            for c in range(8):
                tq_ = tm_["q"][c % 2]
                a_act(tq_, xT[:, c, a:b], AF.Square, [("xT", c, t)], [("lnq", ti, c % 2)])
                yield
                mm(banks[br][:, :], C("ones_ln"), tq_, c == 0, c == 7, ["cons", ("lnq", ti, c % 2)], [("bank", br)])
                yield
            a_copy(lnm_, banks[bm][:, :], [("bank", bm)], [km]); yield
            v_tt(lnr_, lnm_, lnm_, ALU.mult, [km], [kr]); yield
            v_tt(lnr_, banks[br][:, :], lnr_, ALU.subtract, [kr, ("bank", br)], [kr]); yield
            v_ts(lnr_, lnr_, LN_EPS, None, ALU.add, None, [kr], [kr]); yield
            P.op("act", lambda e: e.sqrt(lnr_, lnr_), reads=[kr], writes=[kr]); yield
            P.op("dve", lambda e: e.reciprocal(lnr_, lnr_), reads=[kr], writes=[kr]); yield
            for c in range(8):
                v_tt(xT[:, c, a:b], xT[:, c, a:b], lnm_, ALU.subtract, [("xT", c, t), km], [("xT", c, t)]); yield
                v_tt(xT[:, c, a:b], xT[:, c, a:b], lnr_, ALU.mult, [("xT", c, t), kr], [("xT", c, t)], eng="pool"); yield
                for (eng, dst_fn, sc_fn, bi_fn, wr_fn) in outs:
                    if eng == "act":
                        a_act(dst_fn(c, a, b), xT[:, c, a:b], AF.Identity, [("xT", c, t)], wr_fn(c, t, g),
                              bias=bi_fn(c, g), scale=sc_fn(c, g))
                    else:
                        v_ts(dst_fn(c, a, b), xT[:, c, a:b], sc_fn(c, g), bi_fn(c, g), ALU.mult, ALU.add,
                             [("xT", c, t)], wr_fn(c, t, g), eng=eng)
                    yield

        run_chains([(lambda ti=ti, tl=tl: tile_chain(ti, *tl)) for ti, tl in enumerate(tiles)])
        P.barrier()

    def ln_outs(l, a_idx, b_idx):
        return [
            ("act", lambda c, a, b: hT[:, c, a:b], lambda c, g: drv[:, a_idx, c, g:g + 1],
             lambda c, g: drv[:, b_idx, c, g:g + 1], lambda c, t, g: [("hT", c, g)]),
            ("act", lambda c, a, b: xT[:, c, a:b], lambda c, g: drv[:, 8 + 2 * l, c, 0:1],
             lambda c, g: drv[:, 9 + 2 * l, c, 0:1], lambda c, t, g: [("xT", c, t)]),
        ]

    def ln_final(l):
        return [("act", lambda c, a, b: xT[:, c, a:b], lambda c, g: V("ln_g", l * 8 + c, l * 8 + c + 1),
                 lambda c, g: V("ln_b", l * 8 + c, l * 8 + c + 1), lambda c, t, g: [("xT", c, t)])]

    def write_out(dst, col0, ntok):
        for i in range(ntok // 128):
            a = col0 + i * 128
            xt = xtok[i % 2]
            for half in range(2):
                bk = banks[6 + half]
                for cc in range(4):
                    c = half * 4 + cc
                    P.op("pe", lambda e, bk=bk, cc=cc, c=c, a=a: e.transpose(
                        bk[:, cc * 128:(cc + 1) * 128], xT[:, c, a:a + 128], C("ident")),
                        reads=[("xT", c, a // 512), "cons"], writes=[("bank", 6 + half)])
                if half == 0:
                    a_copy(xt[:, 0:512], bk[:, :], [("bank", 6)], [("xtok", i % 2, 0)])
                else:
                    v_copy(xt[:, 512:1024], bk[:, :], [("bank", 7)], [("xtok", i % 2, 1)])
            dma("sp", dst[i * 128:(i + 1) * 128, :], xt[:, :], [("xtok", i % 2, 0), ("xtok", i % 2, 1)], [])

    def finish():
        P.emit()
        es.close()
        return nc

    def checkpoint(name):
        if DEBUG_STOP == name:
            P.muted = True

    P.barrier()
    modulate(0, 0)
    ffn(0, 1)
    if stage == 1:
        layer_norm(ln_final(0))
        P.barrier()
        write_out(y_out, 0, NT)
        return finish()
    layer_norm(ln_outs(0, 2, 3))
    P.barrier()
    if stage >= 3:
        for hf in range(2):
            dma("sp", ag1_in[hf].rearrange("(c p) t -> p c t", p=128), hT[:, 4 * hf:4 * hf + 4, NP_:NT],
                [("hT", c_, 1) for c_ in range(8)], [("ag1in", hf)])
            P.op("pool", lambda e, hf=hf: e.collective_compute("AllGather", ALU.bypass, replica_groups=RG,
                                                               ins=[ag1_in[hf].opt()], outs=[ag1_out[hf].opt()]),
                 reads=[("ag1in", hf)], writes=[("ag1", hf)], kind="cc")

    TP = [tmpA[0][:, :], tmpA[1][:, :], tmpB[0][:, :], tmpB[1][:, :], lnm[:, :], lnr[:, :],
          xtok[0][:, 0:512], xtok[0][:, 512:1024], xtok[1][:, 0:512], xtok[1][:, 512:1024]]
    TPK = [("tp", i) for i in range(10)]

    ar = Arena()
    yfin_p = ar.bf(8 * 512).rearrange("p (c n) -> p c n", n=512)
    yfs_off = ar.off
    yfin_s = ar.bf(8 * 1024).rearrange("p (c n) -> p c n", n=1024)
    yfwd = yfin_s.rearrange("p c n -> p (c n)").rearrange("p (c n) -> p c n", n=4096)
    ar_scan0 = ar.off

    def BK(b):
        return [("bankc", b, i_) for i_ in range(4)]

    def scan_buffers(ar, ntset_main=2):
        sbuf = {}
        sbuf["wq"] = [ar.bf(8 * 384).rearrange("p (k n) -> p k n", n=384) for _ in range(2)]
        sbuf["raw"] = ar.bf(3 * 640).rearrange("p (c n) -> p c n", n=640)
        sbuf["mixd"] = ar.f32(3 * 512).rearrange("p (c n) -> p c n", n=512)
        sbuf["lor"] = ar.bf(3 * 512).rearrange("p (c n) -> p c n", n=512)
        sbuf["vbd"] = ar.bf(8 * 128).rearrange("p (c n) -> p c n", n=128)
        sbuf["vtok"] = [ar.bf(8 * 128).rearrange("p (c n) -> p c n", n=128) for _ in range(2)]
        sbuf["slot"] = []
        for s in range(2):
            d = {}
            for nm in ("Kt", "Bt"):
                d[nm] = ar.bf(8 * 128).rearrange("p (c n) -> p c n", n=128)
            d["KR"] = ar.bf(8 * 256).rearrange("p (c n) -> p c n", n=256)
            d["Kp"], d["Rt"] = d["KR"][:, :, 0:128], d["KR"][:, :, 128:256]
            d["ydir"] = ar.f32(512)
            d["gam"] = ar.f32(8)
            d["cse"] = ar.f32(16)
            sbuf["slot"].append(d)

        def tset(a_):
            d = {}
            d["Am"] = [a_.bf(512) for _ in range(2)]
            d["PTm"] = a_.bf(128)
            d["SS"] = [a_.bf(256) for _ in range(2)]
            d["Nb"] = [a_.bf(128) for _ in range(2)]
            d["Xs"] = a_.bf(128)
            d["Un"] = a_.bf(128)
            d["T0"] = [a_.bf(128) for _ in range(2)]
            d["KB"] = [a_.bf(256) for _ in range(2)]
            d["Ktok"] = [kb_[:, 0:128] for kb_ in d["KB"]]
            d["Btok"] = [kb_[:, 128:256] for kb_ in d["KB"]]
            return d
        sbuf["tset"] = [tset(ar) for _ in range(2)]
        a2 = Arena()
        a2.off = yfs_off
        sbuf["tset"] += [tset(a2) for _ in range(2)]
        sbuf["Tfin"] = a2.f32(128)
        sbuf["TfT"] = a2.f32(128)
        return sbuf

    smc = [0]

    def shift_mix(*a_):
        for _ in shift_mix_g(*a_):
            pass

    def shift_mix_g(raw, ncols_own, own0, coef, nchunk, dst_fn, mode, seg_first, seg_last, keyr, keyw_fn, j0):
        nd = 3 if mode == "seq" else 5
        for k in range(nchunk):
            par = smc[0] % 2
            smc[0] += 1
            cj = j0 + k
            kD = ("Dbuf", par)
            for dd in range(nd):
                a_act(Dbuf[:, par, dd, :], identb[:, :], AF.Identity, ["identb"], [kD], scale=coef[:, cj, dd:dd + 1])
            yield
            bk = 2 + par
            out = banks[bk][:, 0:ncols_own]
            own = raw[:, k, own0:own0 + ncols_own]
            D = lambda dd: Dbuf[:, par, dd, :]
            rd = [kD, keyr]
            mm(out, D(0), own, True, False, rd, BK(bk))
            if mode == "seq":
                o3 = out.rearrange("p (s t) -> p s t", t=256)
                r3 = own.rearrange("p (s t) -> p s t", t=256)
                mm(o3[:, :, 1:256], D(1), r3[:, :, 0:255], False, False, rd, BK(bk))
                mm(o3[:, :, 0:255], D(2), r3[:, :, 1:256], False, True, rd, BK(bk))
            else:
                o3 = out.rearrange("p (s t) -> p s t", t=64)
                r3 = own.rearrange("p (s t) -> p s t", t=64)
                mm(o3[:, :, 1:64], D(1), r3[:, :, 0:63], False, False, rd, BK(bk))
                mm(o3[:, :, 0:63], D(2), r3[:, :, 1:64], False, False, rd, BK(bk))
                lo = 64 if seg_first else 0
                hi = ncols_own - 64 if seg_last else ncols_own
                mm(out[:, lo:ncols_own], D(3), raw[:, k, own0 + lo - 64:own0 + ncols_own - 64], False, False, rd, BK(bk))
                mm(out[:, 0:hi], D(4), raw[:, k, own0 + 64:own0 + hi + 64], False, True, rd, BK(bk))
            yield
            ev(dst_fn(k), out, BK(bk), [keyw_fn(k)])
            yield

    def scan_prep(B, tag, rm, km, vm, lor, vec, wts, vt, bon_i, dir_slots, TP=TP, TPK=TPK, mk=None, extra_chains=()):
        thw, adb = lor[:, 0, :], lor[:, 1, :]

        def K(*a):
            if mk is not None and a[0] in mk:
                return mk[a[0]]
            return (tag,) + a
        kap, bon, tq = TP[5], TP[bon_i], TP[7]
        kkap, kbon, ktq = TPK[5], TPK[bon_i], TPK[7]
        d_first = dir_slots[0][0]
        bV = 4 + (1 - d_first)

        def chain_kappa():
            a_act(tq, km, AF.Identity, [K("km")], [ktq], scale=vec["k_k"]); yield
            v_tt(kap, tq, tq, ALU.mult, [ktq], [kkap]); yield
            mm(banks[6][:, :], C("blk1"), kap, True, True, ["cons", kkap], BK(6)); yield
            v_ts(kap, banks[6][:, :], 1e-24, None, ALU.max, None, BK(6), [kkap]); yield
            P.op("act", lambda e: e.sqrt(kap, kap), reads=[kkap], writes=[kkap]); yield
            P.op("dve", lambda e: e.reciprocal(kap, kap), reads=[kkap], writes=[kkap]); yield
            v_tt(kap, kap, tq, ALU.mult, [kkap, ktq], [kkap]); yield

        def chain_bonus():
            v_stt(bon, rm, vec["r_k"], km, ALU.mult, ALU.mult, [K("rm"), K("km")], [kbon]); yield
            mm(banks[7][:, :], C("blk1"), bon, True, True, ["cons", kbon], BK(7)); yield
            v_tt(bon, banks[7][:, :], vm, ALU.mult, BK(7) + [K("vm")], [kbon]); yield

        vbd, vtok = B["vbd"], B["vtok"][vt]

        def chain_V():
            for h in range(2):
                src = vm[64 * h:64 * h + 64, :].rearrange("p (c t) -> p c t", t=64)
                dst = vbd[64 * h:64 * h + 64, :, 64 * h:64 * h + 64]
                if h == 0:
                    a_copy(dst, src, [K("vm")], [K("vbd", h)])
                else:
                    v_copy(dst, src, [K("vm")], [K("vbd", h)], eng="pool")
                yield
            for half in range(2):
                for cc in range(4):
                    c = half * 4 + cc
                    mm(banks[bV][:, cc * 128:(cc + 1) * 128], vbd[:, c, :], identb[:, :], True, True,
                       [K("vbd", 0), K("vbd", 1), "identb"], BK(bV))
                yield
                ev(vtok[:, half * 4:half * 4 + 4, :], banks[bV][:, :].rearrange("p (c n) -> p c n", n=128),
                   BK(bV), [K("vtok", vt, half)])
                yield

        sg, cs, lex, aa, bb = TP[0], TP[1], TP[2], TP[3], TP[4]
        ksg, kcs, klex, kaa, kbb = TPK[0:5]

        def chain_dir_early(d, sl):
            S = B["slot"][sl]
            KS = lambda *a: (tag, "s", sl) + a
            pb = 4 + d
            mm(banks[pb][:, :], wts["wup"][d], thw[64 * d:64 * d + 64, :], True, True, [K("lor")], BK(pb)); yield
            a_act(sg, banks[pb][:, :], AF.Sigmoid, BK(pb), [ksg], bias=vec["w0"][d]); yield
            mm(banks[pb][:, :], wts["aup"][d], adb[64 * d:64 * d + 64, :], True, True, [K("lor")], BK(pb)); yield
            a_act(aa, banks[pb][:, :], AF.Sigmoid, BK(pb), [kaa], bias=vec["a0"][d]); yield
            P.op("dve", lambda e: e.tensor_tensor_scan(cs, sg, sg, 0.0, ALU.add, ALU.bypass), reads=[ksg], writes=[kcs])
            yield
            cs3 = cs.rearrange("p (c t) -> p c t", t=64)
            cse, gam = S["cse"], S["gam"]
            P.op("pool", lambda e: e.memset(cse[:, 0:1], 0.0), reads=[], writes=[KS("cse0")])
            v_copy(cse[:, 1:9], cs3[:, :, 63], [kcs], [KS("cse")], eng="pool"); yield
            v_tt(gam, cse[:, 1:9], cse[:, 0:8], ALU.subtract, [KS("cse"), KS("cse0")], [KS("gam")], eng="pool"); yield
            a_act(gam, gam, AF.Exp, [KS("gam")], [KS("gam")], scale=-LAM); yield
            if d == 0:
                v_tt(cs3, cs3, cse[:, 0:8].unsqueeze(2).to_broadcast([128, 8, 64]), ALU.subtract,
                     [kcs, KS("cse"), KS("cse0")], [kcs]); yield
                v_tt(lex, cs, sg, ALU.subtract, [kcs, ksg], [klex]); yield
            else:
                lex3 = lex.rearrange("p (c t) -> p c t", t=64)
                v_tt(lex3, cse[:, 1:9].unsqueeze(2).to_broadcast([128, 8, 64]), cs3, ALU.subtract,
                     [kcs, KS("cse")], [klex]); yield
                v_tt(cs, lex, sg, ALU.add, [klex, ksg], [kcs]); yield
            a_act(sg, cs, AF.Exp, [kcs, klex], [ksg], scale=-LAM); yield
            a_act(lex, lex, AF.Exp, [klex], [klex], scale=-LAM); yield
            a_act(cs, cs, AF.Exp, [kcs, ksg], [kcs], scale=LAM); yield

        def bd_build(S, KS, nm, x_, e_, kx, ke, engs):
            for h in range(2):
                dst = S[nm][64 * h:64 * h + 64, :, 64 * h:64 * h + 64]
                a_ = x_[64 * h:64 * h + 64, :].rearrange("p (c t) -> p c t", t=64)
                b_ = e_[64 * h:64 * h + 64, :].rearrange("p (c t) -> p c t", t=64)
                v_tt(dst, a_, b_, ALU.mult, [kx, ke], [KS(nm, h)], eng=engs[h])
                yield

        def dir_late_a(d, sl):
            S = B["slot"][sl]
            KS = lambda *a: (tag, "s", sl) + a
            for _ in bd_build(S, KS, "Rt", rm, sg, K("rm"), ksg, ("pool", "pool")):
                yield
            for _ in bd_build(S, KS, "Kp", kap, lex, kkap, klex, ("pool", "pool")):
                yield

        def dir_late_b(d, sl):
            S = B["slot"][sl]
            KS = lambda *a: (tag, "s", sl) + a
            v_tt(bb, kap, aa, ALU.mult, [kkap, kaa], [kbb]); yield
            v_ts(aa, aa, -1.0, vec["k_a"], ALU.add, ALU.mult, [kaa, kbb], [kaa]); yield
            v_stt(aa, aa, 1.0, km, ALU.add, ALU.mult, [kaa, K("km")], [kaa]); yield
            for _ in bd_build(S, KS, "Bt", bb, cs, kbb, kcs, ("dve", "pool")):
                yield
            for _ in bd_build(S, KS, "Kt", aa, cs, kaa, kcs, ("dve", "pool")):
                yield

        d0, s0 = dir_slots[0]
        carry = [c() for c in extra_chains]
        carry = run_chains([chain_kappa, chain_bonus, chain_V, lambda: chain_dir_early(d0, s0)], carry)
        carry = run_chains([lambda: dir_late_a(d0, s0), lambda: dir_late_b(d0, s0)], carry)
        for (d, sl) in dir_slots[1:]:
            carry = run_chains([lambda: chain_dir_early(d, sl)], carry)
            carry = run_chains([lambda: dir_late_a(d, sl), lambda: dir_late_b(d, sl)], carry)
        for g_ in carry:
            for _ in g_:
                pass

    def scan_chunks(B, tag, groups):
        maxlen = max(len(g["chunks"]) for g in groups)
        for g in groups:
            g["cur"] = 0
        for step in range(maxlen):
            act_g = [g for g in groups if step < len(g["chunks"])]
            ctx = []
            for g in act_g:
                d, sl = g["dir"], g["slot"]
                S, T = B["slot"][sl], B["tset"][g["tset"]]
                x = {"g": g, "S": S, "T": T, "d": d, "c": g["chunks"][step]}
                x["KS"] = (lambda sl: (lambda *a: (tag, "s", sl) + a))(sl)
                x["KT"] = (lambda ts: (lambda *a: (tag, "t", ts) + a))(g["tset"])
                c = x["c"]
                x["Kt"], x["Bt"], x["Kp"], x["Rt"] = S["Kt"][:, c, :], S["Bt"][:, c, :], S["Kp"][:, c, :], S["Rt"][:, c, :]
                for nm in ("Kt", "Bt", "Kp", "Rt"):
                    x["k" + nm] = [x["KS"](nm, 0), x["KS"](nm, 1)]
                x["KR"] = S["KR"][:, c, :]
                x["vtok"] = B["vtok"][g["vt"]][:, c, :]
                x["VT"] = [(tag, "vtok", g["vt"], 0), (tag, "vtok", g["vt"], 1)]
                x["bx"], x["by"] = g["banks"]
                x["msk"] = "f" if d == 0 else "b"
                x["pA"] = step % 2
                ctx.append(x)

            def st_init(x):
                g, T, KT = x["g"], x["T"], x["KT"]
                c = x["c"]
                first = (step == 0) or (g.get("restart") and c in g["restart"])
                if first:
                    cur = g["cur"]
                    T0 = T["T0"][cur]
                    if g["T0init"] is None:
                        P.op("pool", lambda e, T0=T0: e.memset(T0, 0.0), reads=[], writes=[KT("T0", cur)])
                    elif step == 0:
                        g["T0init"](T0, KT("T0", cur))
                x["cur"] = g["cur"]
                x["T0"] = T["T0"][x["cur"]]
                x["kT0"] = KT("T0", x["cur"])

            def st_A(x):
                bx, by = x["bx"], x["by"]
                mm(banks[bx][:, 0:256], x["Kt"], x["KR"], True, True, x["kKt"] + x["kKp"] + x["kRt"], BK(bx))
                mm(banks[bx][:, 256:512], x["Bt"], x["KR"], True, True, x["kBt"] + x["kKp"] + x["kRt"], BK(bx))
                mm(banks[by][:, 0:128], x["Kp"], x["Bt"], True, True, x["kKp"] + x["kBt"], BK(by))
                mm(banks[by][:, 128:256], x["Kt"], identb[:, :], True, True, x["kKt"] + ["identb"], BK(by))
                mm(banks[by][:, 256:384], x["Bt"], identb[:, :], True, True, x["kBt"] + ["identb"], BK(by))

            def st_Aev(x):
                T, KT, pA = x["T"], x["KT"], x["pA"]
                bx, by = x["bx"], x["by"]
                Am = T["Am"][pA]
                x["Am"], x["kAm"] = Am, KT("Am", pA)
                v_tt(Am, banks[bx][:, :], CM("maskA_" + x["msk"]), ALU.mult, BK(bx) + ["cons"], [x["kAm"]])
                v_tt(T["PTm"], banks[by][:, 0:128], CM("maskPT_" + x["msk"]), ALU.mult, BK(by) + ["cons"], [KT("PTm")])
                x["Ktok"], x["Btok"] = T["Ktok"][pA], T["Btok"][pA]
                a_copy(T["KB"][pA], banks[by][:, 128:384], BK(by), [KT("Ktok", pA), KT("Btok", pA)])
                v_tt(T["Nb"][0], identb[:, :], Am[:, 256:384], ALU.subtract, ["identb", x["kAm"]], [KT("Nb", 0)], eng="pool")
                x["Sc"], x["STc"], x["kS"], x["ncur"] = Am[:, 256:384], T["PTm"], [x["kAm"], KT("PTm")], 0

            def mk_series(j):
                def st_S(x):
                    T, KT, bx = x["T"], x["KT"], x["bx"]
                    half = (j % 2) * 256
                    mm(banks[bx][:, half + 128:half + 256], x["Sc"], x["STc"], True, True, x["kS"], BK(bx))
                    if j < 5:
                        mm(banks[bx][:, half:half + 128], x["STc"], x["Sc"], True, True, x["kS"], BK(bx))

                def st_Sev(x):
                    T, KT, bx = x["T"], x["KT"], x["bx"]
                    half = (j % 2) * 256
                    SS = T["SS"][j % 2]
                    kSS = KT("SS", j % 2)
                    if j < 5:
                        a_copy(SS, banks[bx][:, half:half + 256], BK(bx), [kSS])
                    else:
                        a_copy(SS[:, 128:256], banks[bx][:, half + 128:half + 256], BK(bx), [kSS])
                    x["Sc"], x["STc"], x["kS"] = SS[:, 0:128], SS[:, 128:256], [kSS]

                def st_N(x):
                    T, KT, by = x["T"], x["KT"], x["by"]
                    Nprev = T["Nb"][x["ncur"]]
                    if j == 1:
                        mm(banks[by][:, 384:512], identb[:, :], Nprev, True, False, ["identb", KT("Nb", x["ncur"])], BK(by))
                    mm(banks[by][:, 384:512], x["STc"], Nprev, False, (j == 5), x["kS"] + [KT("Nb", x["ncur"])], BK(by))

                def st_Nev(x):
                    T, KT, by = x["T"], x["KT"], x["by"]
                    v_copy(T["Nb"][1 - x["ncur"]], banks[by][:, 384:512], BK(by), [KT("Nb", 1 - x["ncur"])])
                    x["ncur"] = 1 - x["ncur"]
                return [st_S, st_Sev, st_N, st_Nev]

            def st_X(x):
                by = x["by"]
                mm(banks[by][:, 0:128], x["Kp"], x["T0"], True, False, x["kKp"] + [x["kT0"]], BK(by))
                mm(banks[by][:, 0:128], x["Am"][:, 0:128], x["vtok"], False, True, [x["kAm"]] + x["VT"], BK(by))

            def st_Xev(x):
                a_copy(x["T"]["Xs"], banks[x["by"]][:, 0:128], BK(x["by"]), [x["KT"]("Xs")])

            def st_U(x):
                T, KT, bx = x["T"], x["KT"], x["bx"]
                mm(banks[bx][:, 128:256], T["Nb"][x["ncur"]], T["Xs"], True, True, [KT("Nb", x["ncur"]), KT("Xs")], BK(bx))

            def st_Uev(x):
                a_copy(x["T"]["Un"], banks[x["bx"]][:, 128:256], BK(x["bx"]), [x["KT"]("Un")], scale=-1.0)

            def st_T(x):
                T, KT, bx, pA = x["T"], x["KT"], x["bx"], x["pA"]
                mm(banks[bx][:, 256:384], x["Ktok"], x["vtok"], True, False, [KT("Ktok", pA)] + x["VT"], BK(bx))
                mm(banks[bx][:, 256:384], x["Btok"], T["Un"], False, False, [KT("Btok", pA), KT("Un")], BK(bx))
                mm(banks[bx][:, 256:384], identb[:, :], x["T0"], False, True, ["identb", x["kT0"]], BK(bx))
                mm(banks[bx][:, 384:512], x["vtok"], x["Am"][:, 128:256], True, False, x["VT"] + [x["kAm"]], BK(bx))
                mm(banks[bx][:, 384:512], T["Un"], x["Am"][:, 384:512], False, False, [KT("Un"), x["kAm"]], BK(bx))
                mm(banks[bx][:, 384:512], x["T0"], x["Rt"], False, True, [x["kT0"]] + x["kRt"], BK(bx))

            def st_Tev(x):
                g, S, T, KT, KS, bx, c = x["g"], x["S"], x["T"], x["KT"], x["KS"], x["bx"], x["c"]
                cur = x["cur"]
                if g.get("final_at") and c in g["final_at"]:
                    g["final"](c, banks[bx][:, 256:384], ("bank", bx), S["gam"][:, c:c + 1], KS("gam"), x["by"])
                nxt = 1 - cur
                nxt_restart = (step + 1 < len(g["chunks"])) and g.get("restart") and g["chunks"][step + 1] in g["restart"]
                if not nxt_restart:
                    a_copy(T["T0"][nxt], banks[bx][:, 256:384], BK(bx) + [KS("gam")], [KT("T0", nxt)],
                           scale=S["gam"][:, c:c + 1])
                g["ysink"](c, banks[bx], ("bank", bx))
                g["cur"] = nxt

            stages = [st_init, st_A, st_Aev, st_X, st_Xev]
            for j in range(1, 6):
                stages += mk_series(j)
            stages += [st_U, st_Uev, st_T, st_Tev]
            for st in stages:
                for x in ctx:
                    st(x)

    def finalize_y(*a_, **kw):
        for _ in finalize_y_g(*a_, **kw):
            pass

    def finalize_y_g(B, tag, ysum, kysum, bon_key, lor, gup_ap, lnxg, lnxb, dst, kdst, ykeys=None, gkey="wl", bon_i=6,
                     bon_ap=None, tmp=(0, 1, 2), bk=(4, 5), TPx=None, TPKx=None):
        TPx = TP if TPx is None else TPx
        TPKx = TPK if TPKx is None else TPKx
        sgd = lor[:, 2, :]
        bon = TPx[bon_i] if bon_ap is None else bon_ap
        t1, t2, t3 = TPx[tmp[0]], TPx[tmp[1]], TPx[tmp[2]]
        k1, k2, k3 = TPKx[tmp[0]], TPKx[tmp[1]], TPKx[tmp[2]]
        b0, b1 = bk
        yk = ykeys if ykeys is not None else [kysum]
        mm(banks[b0][:, :], C("blk64"), ysum, True, True, ["cons"] + yk, BK(b0)); yield
        v_tt(t1, ysum, ysum, ALU.mult, yk, [k1]); yield
        mm(banks[b1][:, :], C("blk64"), t1, True, True, ["cons", k1], BK(b1)); yield
        a_copy(t2, banks[b0][:, :], BK(b0), [k2]); yield
        v_tt(t3, t2, t2, ALU.mult, [k2], [k3]); yield
        v_tt(t3, banks[b1][:, :], t3, ALU.subtract, BK(b1) + [k3], [k3]); yield
        v_ts(t3, t3, GN_EPS, None, ALU.add, None, [k3], [k3]); yield
        P.op("act", lambda e: e.sqrt(t3, t3), reads=[k3], writes=[k3]); yield
        P.op("dve", lambda e: e.reciprocal(t3, t3), reads=[k3], writes=[k3]); yield
        v_tt(t1, ysum, t2, ALU.subtract, yk + [k2, k1], [k1]); yield
        v_tt(t1, t1, t3, ALU.mult, [k1, k3], [k1]); yield
        a_act(t1, t1, AF.Identity, [k1], [k1], bias=lnxb, scale=lnxg); yield
        v_tt(t1, t1, bon, ALU.add, [k1, bon_key], [k1]); yield
        mm(banks[b0][:, :], gup_ap, sgd, True, True, [(tag, "lor"), gkey], BK(b0)); yield
        v_tt(dst, t1, banks[b0][:, :], ALU.mult, [k1] + BK(b0), [kdst]); yield

    BMx = lambda d: 2 + d

    def load_wtile(wq, src_cols, key):
        dma("pool", wq, src_cols, [], [key])

    def project3(*a_):
        for _ in project3_g(*a_):
            pass

    def project3_g(wq, wkey, hsrc_fn, hkeys, segs, raw, rawkey):
        for k in range(3):
            for (c0, ncol, bk) in segs:
                for kc in range(8):
                    mm(banks[bk][:, 0:ncol], wq[:, kc, k * 128:(k + 1) * 128], hsrc_fn(kc, c0, ncol),
                       kc == 0, kc == 7, [wkey] + hkeys, BK(bk))
                yield
                ev(raw[:, k, c0:c0 + ncol], banks[bk][:, 0:ncol], BK(bk), [rawkey])
                yield

    coef_p = sb("coef_p", [128, 27, 3], F32)
    coef_s = sb("coef_s", [128, 9, 5], F32)
    for (cf, mu, n, masks) in ((coef_p, V("mu_pp"), 27, ("mP", "mN")), (coef_s, V("mu_ss"), 9, ("mL", "mR", "mU", "mD"))):
        v_ts(cf[:, :, 0], mu, -1.0, 1.0, ALU.mult, ALU.add, ["vecs"], [("coef", n)])
        for i, m in enumerate(masks):
            v_ts(cf[:, :, i + 1], mu, C(m), None, ALU.mult, None, ["vecs", "cons"], [("coef", n)])

    ar.off = ar_scan0
    B = scan_buffers(ar)
    wl = ar.bf(3 * 1024).rearrange("p (c n) -> p c n", n=1024)
    dma("pool", wl[:, 0, :], wup_d, [], ["wl"])
    dma("pool", wl[:, 1, :], aup_d, [], ["wl"])
    dma("pool", wl[:, 2, :], gup_d, [], ["wl"])
    for s in range(2):
        for nm in ("Kt", "Bt"):
            P.op("dve", lambda e, t=B["slot"][s][nm]: e.memset(t, 0.0), reads=[], writes=[("P", "s", s, nm, 0), ("P", "s", s, nm, 1)])
        P.op("dve", lambda e, t=B["slot"][s]["KR"]: e.memset(t, 0.0), reads=[],
             writes=[("P", "s", s, nm_, h_) for nm_ in ("Kp", "Rt") for h_ in range(2)])
    P.op("dve", lambda e: e.memset(B["vbd"], 0.0), reads=[], writes=[("P", "vbd", 0), ("P", "vbd", 1)])
    P.barrier()

    wpp = w_mix_pp.rearrange("(kc p) n -> p kc n", p=128)
    lor = B["lor"]
    _off = ar.off
    rawL = [B["raw"], ar.bf(3 * 640).rearrange("p (c n) -> p c n", n=640)]
    mixL = [B["mixd"], ar.f32(3 * 512).rearrange("p (c n) -> p c n", n=512)]
    ar.off = _off
    MK = [{nm: ("P", nm, i_) for nm in ("rm", "km", "vm")} for i_ in range(2)]

    def h_prompt(kc, c0, ncol):
        return hT[:, kc, c0:c0 + ncol]

    HP = [("hT", c_, 0) for c_ in range(8)]
    load_wtile(B["wq"][0], wpp[:, :, 0:384], ("wq", 0))
    project3(B["wq"][0], ("wq", 0), h_prompt, HP, [(0, 512, 0)], rawL[0], ("raw", 0))
    shift_mix(rawL[0], 512, 0, coef_p, 3, lambda k: mixL[0][:, k, :], "seq", False, False, ("raw", 0),
              lambda k: MK[0][("rm", "km", "vm")[k]], 0)
    a_act(lor[:, 0, :], mixL[0][:, 0, :], AF.Tanh, [MK[0]["rm"]], [("P", "lor")])
    a_copy(lor[:, 1, :], mixL[0][:, 1, :], [MK[0]["km"]], [("P", "lor")])
    a_act(lor[:, 2, :], mixL[0][:, 2, :], AF.Sigmoid, [MK[0]["vm"]], [("P", "lor")])

    TfT = B["TfT"]
    Tfin = B["Tfin"]
    checkpoint("lora")

    def prompt_proj_g(pr):
        bi = (pr + 1) % 2
        wq = B["wq"][bi]
        load_wtile(wq, wpp[:, :, 384 * (pr + 1):384 * (pr + 2)], ("wq", bi))
        yield
        for _ in project3_g(wq, ("wq", bi), h_prompt, HP, [(0, 512, pr % 2)], rawL[bi], ("raw", bi)):
            yield
        for _ in shift_mix_g(rawL[bi], 512, 0, coef_p, 3, lambda k: mixL[bi][:, k, :], "seq", False, False,
                             ("raw", bi), lambda k: MK[bi][("rm", "km", "vm")[k]], 3 + 3 * pr):
            yield

    for _ in prompt_proj_g(0):
        pass
    for pr in range(8):
        bi = (pr + 1) % 2
        mixd = mixL[bi]
        tag = "P"
        nxt = [(lambda pr=pr: prompt_proj_g(pr + 1))] if pr < 7 else []
        vec = {"k_k": V("k_k", pr, pr + 1), "k_a": V("k_a", pr, pr + 1), "r_k": V("r_k", pr, pr + 1),
               "w0": [V("w0", d * 8 + pr, d * 8 + pr + 1) for d in range(2)],
               "a0": [V("a0", d * 8 + pr, d * 8 + pr + 1) for d in range(2)]}
        wts = {"wup": [wl[64 * d:64 * d + 64, 0, pr * 128:(pr + 1) * 128] for d in range(2)],
               "aup": [wl[64 * d:64 * d + 64, 1, pr * 128:(pr + 1) * 128] for d in range(2)]}

        def mk_ysink(d):
            S = B["slot"][d]

            def ysink(c, bank, bkey):
                for h in range(2):
                    dst = S["ydir"][64 * h:64 * h + 64, c * 64:(c + 1) * 64]
                    src = bank[64 * h:64 * h + 64, 384 + 64 * h:384 + 64 * h + 64]
                    if h == 0:
                        a_copy(dst, src, [bkey], [("P", "ydir", d, h)])
                    else:
                        v_copy(dst, src, [bkey], [("P", "ydir", d, h)])
            return ysink

        def mk_final(d, pr):
            def final(c, psrc, pkey, gam, gkey, by):
                seq = c // 4
                a_copy(Tfin, psrc, [pkey, gkey], ["Tfin"], scale=gam)
                P.op("pe", lambda e: e.transpose(banks[by][:, 384:512], Tfin, C("ident")),
                     reads=["Tfin", "cons"], writes=[("bank", by)])
                v_copy(TfT, banks[by][:, 384:512], [("bank", by)], ["TfT"])
                for h in range(2):
                    row0 = ((seq * 2 + d) * 16 + 2 * pr + h) * 64
                    dma("sp", st_out[row0:row0 + 64, :], TfT[64 * h:64 * h + 64, 64 * h:64 * h + 64], ["TfT"], [])
            return final

        scan_prep(B, tag, mixd[:, 0, :], mixd[:, 1, :], mixd[:, 2, :], lor, vec, wts, 0, 6, [(0, 0), (1, 1)],
                  mk=MK[bi], extra_chains=nxt)
        groups = []
        for d in range(2):
            for seq in range(2):
                chunks = list(range(4 * seq, 4 * seq + 4))
                if d == 1:
                    chunks = chunks[::-1]
                gi_ = 2 * d + seq
                groups.append({"dir": d, "slot": d, "tset": gi_, "vt": 0, "banks": (2 * gi_, 2 * gi_ + 1), "chunks": chunks,
                               "T0init": None, "restart": None, "final_at": {chunks[-1]}, "final": mk_final(d, pr),
                               "ysink": mk_ysink(d)})
        scan_chunks(B, tag, groups)
        ys = TP[8]
        v_tt(ys, B["slot"][0]["ydir"], B["slot"][1]["ydir"], ALU.add,
             [("P", "ydir", 0, 0), ("P", "ydir", 0, 1), ("P", "ydir", 1, 0), ("P", "ydir", 1, 1)], [TPK[8]])
        finalize_y(B, tag, ys, TPK[8], TPK[6], lor, wl[:, 2, pr * 128:(pr + 1) * 128],
                   V("lnx_g", pr, pr + 1), V("lnx_b", pr, pr + 1), yfin_p[:, pr, :], ("yfin", pr, 0))
        checkpoint("pair")

    def sample_mixer():
        P.barrier()
        xTf = xT[:, :, :].rearrange("p c n -> p (c n)")
        dma("sp", xsave, xTf, [("xT", c_, t_) for c_ in range(8) for t_ in range(3)], ["xsave"])
        P.barrier()
        xr = Arena(xTf, 8 * NT)

        def tset(a_):
            d = {}
            d["Am"] = [a_.bf(512) for _ in range(2)]
            d["PTm"] = a_.bf(128)
            d["SS"] = [a_.bf(256) for _ in range(2)]
            d["Nb"] = [a_.bf(128) for _ in range(2)]
            d["Xs"] = a_.bf(128)
            d["Un"] = a_.bf(128)
            d["T0"] = [a_.bf(128) for _ in range(2)]
            d["KB"] = [a_.bf(256) for _ in range(2)]
            d["Ktok"] = [kb_[:, 0:128] for kb_ in d["KB"]]
            d["Btok"] = [kb_[:, 128:256] for kb_ in d["KB"]]
            return d

        Bs = dict(B)
        Bs["slot"] = list(B["slot"])
        for s_ in range(2):
            d = {}
            for nm in ("Kt", "Bt"):
                d[nm] = xr.bf(8 * 128).rearrange("p (c n) -> p c n", n=128)
            d["KR"] = xr.bf(8 * 256).rearrange("p (c n) -> p c n", n=256)
            d["Kp"], d["Rt"] = d["KR"][:, :, 0:128], d["KR"][:, :, 128:256]
            d["ydir"] = xr.f32(512)
            d["gam"] = xr.f32(8)
            d["cse"] = xr.f32(16)
            Bs["slot"].append(d)
        Bs["vtok"] = list(B["vtok"]) + [xr.bf(8 * 128).rearrange("p (c n) -> p c n", n=128) for _ in range(2)]
        Bs["tset"] = [B["tset"][0], B["tset"][1], tset(xr), tset(xr)]
        lorv = [B["lor"], xr.bf(3 * 512).rearrange("p (c n) -> p c n", n=512)]
        TPs = list(TP) + [xr.f32(512), xr.f32(512)]
        TPKs = list(TPK) + [("tp", 10), ("tp", 11)]
        BON = [6, 8, 10, 11]
        hs = ar.bf(8 * 640).rearrange("p (k n) -> p k n", n=640)
        wq3 = [wl.rearrange("p c n -> p (c n)").rearrange("p (k n) -> p k n", n=384), B["wq"][0], B["wq"][1]]
        wls = ar.bf(3 * 256).rearrange("p (c n) -> p c n", n=256)
        tcv = TP[9][:, 256:512].bitcast(BF16)
        Tcar = [[tcv[:, (2 * pl_ + d_) * 128:(2 * pl_ + d_ + 1) * 128] for d_ in range(2)] for pl_ in range(2)]
        ybuf = TP[9].bitcast(BF16)[:, 0:512]
        s0tmp = TP[8][:, 0:64]
        wss = w_mix_ss.rearrange("(kc p) n -> p kc n", p=128)
        for i in range(3):
            dma("pool", wq3[i], wss[:, :, 384 * i:384 * (i + 1)], [], [("wq3", i)])
        dma("pool", wls[:, 0, :], wup_sd, [], ["wls"])
        dma("pool", wls[:, 1, :], aup_sd, [], ["wls"])
        dma("pool", wls[:, 2, :], gup_sd, [], ["wls"])
        P.op("dve", lambda e: e.memset(hs, 0.0), reads=[], writes=["hs"])
        for s_ in range(4):
            for nm in ("Kt", "Bt"):
                P.op("dve", lambda e, t=Bs["slot"][s_][nm]: e.memset(t, 0.0), reads=[],
                     writes=[("P", "s", s_, nm, 0), ("P", "s", s_, nm, 1)])
            P.op("dve", lambda e, t=Bs["slot"][s_]["KR"]: e.memset(t, 0.0), reads=[],
                 writes=[("P", "s", s_, nm_, h_) for nm_ in ("Kp", "Rt") for h_ in range(2)])
        P.op("dve", lambda e: e.memset(B["vbd"], 0.0), reads=[], writes=[("P", "vbd", 0), ("P", "vbd", 1)])
        for pl in range(2):
            for d in range(2):
                tc = Tcar[pl][d]
                P.op("pool", lambda e, tc=tc: e.memset(tc, 0.0), reads=[], writes=[("Tcar", pl, d)])
                row0 = (d * 4 + 2 * pl) * 64
                dma("sp", s0tmp, s0T_d[row0:row0 + 128, :], [], [TPK[8]])
                for h in range(2):
                    v_copy(tc[64 * h:64 * h + 64, 64 * h:64 * h + 64], s0tmp[64 * h:64 * h + 64, :], [TPK[8]],
                           [("Tcar", pl, d)], eng="pool")
        agv = [ag1_out[hf].rearrange("(r c p) t -> p r c t", p=128, c=4) for hf in range(2)]
        tag = "P"
        yp_ar = Arena()
        rawL = [B["raw"], xr.bf(3 * 640).rearrange("p (c n) -> p c n", n=640)]
        mixL = [B["mixd"], yp_ar.f32(3 * 512).rearrange("p (c n) -> p c n", n=512)]
        MKs = [{nm: ("P", nm, i_) for nm in ("rm", "km", "vm")} for i_ in range(2)]

        def h_seg(kc, c0, ncol):
            return hs[:, kc, c0:c0 + ncol]

        def sample_proj_g(i, d, pl, bi):
            s = i if d == 0 else 7 - i
            lor = lorv[d]
            raw, mixd, mk = rawL[bi], mixL[bi], MKs[bi]
            t0, t1 = max(512 * s - 64, 0), min(512 * s + 576, 4096)
            w0 = 512 * s - 64
            c_lo, c_hi = t0 - w0, t1 - w0
            pieces = [(c_lo, min(512, c_hi - c_lo), 0)]
            if c_hi - c_lo > 512:
                pieces.append((c_lo + 512, c_hi - c_lo - 512, 1))
            first, last = (s == 0), (s == 7)
            mkf = lambda k: mk[("rm", "km", "vm")[k]]
            if pl == 0:
                tt_ = t0
                while tt_ < t1:
                    r = tt_ // 1024
                    te = min(t1, (r + 1) * 1024)
                    for hf in range(2):
                        dma("sp", hs[:, 4 * hf:4 * hf + 4, tt_ - w0:te - w0], agv[hf][:, r, :, tt_ - r * 1024:te - r * 1024],
                            [("ag1", hf)], ["hs"])
                    tt_ = te
                yield
                for _ in project3_g(wq3[0], ("wq3", 0), h_seg, ["hs"], pieces, raw, ("raw", bi)):
                    yield
                for _ in shift_mix_g(raw, 512, 64, coef_s, 3, lambda k: mixd[:, k, :], "grid", first, last, ("raw", bi),
                                     mkf, 0):
                    yield
                a_act(lor[:, 0, :], mixd[:, 0, :], AF.Tanh, [mk["rm"]], [(tag, "lor")])
                a_copy(lor[:, 1, :], mixd[:, 1, :], [mk["km"]], [(tag, "lor")])
                a_act(lor[:, 2, :], mixd[:, 2, :], AF.Sigmoid, [mk["vm"]], [(tag, "lor")])
                yield
            for _ in project3_g(wq3[1 + pl], ("wq3", 1 + pl), h_seg, ["hs"], pieces, raw, ("raw", bi)):
                yield
            for _ in shift_mix_g(raw, 512, 64, coef_s, 3, lambda k: mixd[:, k, :], "grid", first, last, ("raw", bi),
                                 mkf, 3 + 3 * pl):
                yield

        pre_done = [False]
        for i in range(8):
            groups = []
            fin = []
            combos = [(0, 0), (0, 1), (1, 0), (1, 1)]
            if not pre_done[0]:
                for _ in sample_proj_g(i, 0, 0, 0):
                    pass
            pre_done[0] = False
            for ci, (d, pl) in enumerate(combos):
                bi = ci % 2
                mixd, mk = mixL[bi], MKs[bi]
                s = i if d == 0 else 7 - i
                lor = lorv[d]
                if True:
                    sl = 2 * d + pl
                    vec = {"k_k": V("k_k_s", pl, pl + 1), "k_a": V("k_a_s", pl, pl + 1), "r_k": V("r_k_s", pl, pl + 1),
                           "w0": [V("w0_s", dd * 2 + pl, dd * 2 + pl + 1) for dd in range(2)],
                           "a0": [V("a0_s", dd * 2 + pl, dd * 2 + pl + 1) for dd in range(2)]}
                    wts = {"wup": [wls[64 * dd:64 * dd + 64, 0, pl * 128:(pl + 1) * 128] for dd in range(2)],
                           "aup": [wls[64 * dd:64 * dd + 64, 1, pl * 128:(pl + 1) * 128] for dd in range(2)]}
                    tc = Tcar[pl][d]
                    ys = Bs["slot"][sl]["ydir"]
                    second = (i >= 4)

                    def ysink(c, bank, bkey, pl=pl, s=s, sl=sl, ys=ys, second=second):
                        for h in range(2):
                            src = bank[64 * h:64 * h + 64, 384 + 64 * h:384 + 64 * h + 64]
                            yf = yfwd[64 * h:64 * h + 64, pl, s * 512 + c * 64:s * 512 + (c + 1) * 64]
                            if not second:
                                if h == 0:
                                    a_copy(yf, src, [bkey], [("yst", pl, s, h)])
                                else:
                                    v_copy(yf, src, [bkey], [("yst", pl, s, h)])
                            else:
                                v_tt(ys[64 * h:64 * h + 64, c * 64:(c + 1) * 64], src, yf, ALU.add,
                                     [bkey, ("yst", pl, s, h)], [("ysum", sl, h)])

                    def t0init(T0, key, tc=tc, pl=pl, d=d):
                        v_copy(T0, tc, [("Tcar", pl, d)], [key], eng="pool")

                    def final(c, psrc, pkey, gam, gkey, by, tc=tc, pl=pl, d=d):
                        a_copy(tc, psrc, [pkey, gkey], [("Tcar", pl, d)], scale=gam)

                    chunks = list(range(8)) if d == 0 else list(range(7, -1, -1))
                    nxt = []
                    if ci < 3:
                        d2, pl2 = combos[ci + 1]
                        nxt = [(lambda d2=d2, pl2=pl2, ci=ci: sample_proj_g(i, d2, pl2, (ci + 1) % 2))]
                    elif i < 4:
                        nxt = [(lambda: sample_proj_g(i + 1, 0, 0, 0))]
                        pre_done[0] = True
                    scan_prep(Bs, tag, mixd[:, 0, :], mixd[:, 1, :], mixd[:, 2, :], lor, vec, wts, sl, BON[sl],
                              [(d, sl)], TPs, TPKs, mk=mk, extra_chains=nxt)
                    groups.append({"dir": d, "slot": sl, "tset": sl, "vt": sl, "banks": (2 * sl, 2 * sl + 1),
                                   "chunks": chunks, "T0init": t0init, "restart": None, "final_at": {chunks[-1]},
                                   "final": final, "ysink": ysink})
                    if second:
                        fin.append((d, pl, sl, s, lor))
            scan_chunks(Bs, tag, groups)
            ybufs = [ybuf, TPs[7].bitcast(BF16)[:, 0:512]]
            for f0 in range(0, len(fin), 2):
                chains = []
                for fi, (d, pl, sl, s, lor) in enumerate(fin[f0:f0 + 2]):
                    def one(d=d, pl=pl, sl=sl, s=s, lor=lor, fi=fi):
                        yb = ybufs[fi]
                        for _ in finalize_y_g(Bs, tag, Bs["slot"][sl]["ydir"], None, TPKs[BON[sl]], lor,
                                              wls[:, 2, pl * 128:(pl + 1) * 128], V("lnx_g_s", pl, pl + 1),
                                              V("lnx_b_s", pl, pl + 1), yb, (("ybuf", 0), TPKs[7])[fi],
                                              ykeys=[("ysum", sl, 0), ("ysum", sl, 1)], gkey="wls", bon_ap=TPs[BON[sl]],
                                              tmp=((0, 1, 2), (3, 4, 5))[fi], bk=((4, 5), (6, 7))[fi], TPx=TPs, TPKx=TPKs):
                            yield
                        dma("sp", ag2_in[pl][:, s * 512:(s + 1) * 512], yb, [(("ybuf", 0), TPKs[7])[fi]], [("ag2in", pl)])
                        yield
                    chains.append(one)
                carry = []
                if f0 == 2 and i < 7:
                    carry = [sample_proj_g(i + 1, 0, 0, 0)]
                    pre_done[0] = True
                carry = run_chains(chains, carry)
                for g_ in carry:
                    for _ in g_:
                        pass
        P.barrier()
        dma("sp", xTf, xsave, ["xsave"], [("xT", c_, t_) for c_ in range(8) for t_ in range(3)])

    def merge_phase(tiles, before_ya=None):
        P.barrier()
        am = Arena()
        am.off = ar_scan0
        worw = am.bf(8 * 1024).rearrange("p (c n) -> p c n", n=1024)
        wout = worw
        wopl = am.bf(4 * 1024).rearrange("p (c n) -> p c n", n=1024)
        wpl = am.bf(4 * 128).rearrange("p (c n) -> p c n", n=128)
        wg = [am.bf(8 * 256).rearrange("p (k n) -> p k n", n=256) for _ in range(2)]
        gates = am.bf(16 * 512).rearrange("p (c n) -> p c n", n=512)
        merged = am.bf(8 * 512).rearrange("p (c n) -> p c n", n=512)
        ub = am.bf(4 * 512).rearrange("p (c n) -> p c n", n=512)
        pinA = am.f32(4 * 544).rearrange("p (c n) -> p c n", n=544)
        pW = [am.f32(544) for _ in range(2)]
        hal = am.bf(8 * 16).rearrange("p (k n) -> p k n", n=16)
        halall = am.bf(4 * 8 * 16).rearrange("p (r k n) -> p r k n", k=8, n=16)
        if any(t_[0] > 0 for t_ in tiles):
            for hf in range(2):
                agh = ag1_out[hf].rearrange("(r c p) t -> p r c t", p=128, c=4)
                for r_ in range(4):
                    dma("sp", halall[:, r_, 4 * hf:4 * hf + 4, 0:8], agh[:, r_, :, 1016:1024], [("ag1", hf)], ["halall"])
                    dma("sp", halall[:, r_, 4 * hf:4 * hf + 4, 8:16], agh[:, r_, :, 0:8], [("ag1", hf)], ["halall"])

        def pick_halo(dst, lo, oh):
            v_ts(dst, halall[:, 0, :, lo:lo + 8], oh[:, 0:1], None, ALU.mult, None, ["halall", "vecs"], ["hal"])
            for r_ in range(1, 4):
                v_stt(dst, halall[:, r_, :, lo:lo + 8], oh[:, r_:r_ + 1], dst, ALU.mult, ALU.add, ["halall", "vecs", "hal"], ["hal"])
        dma("pool", wopl, wopool_d.rearrange("(c p) n -> p c n", p=128), [], ["wopl"])
        dma("pool", wpl, wpool_d.rearrange("(c p) n -> p c n", p=128), [], ["wpl"])
        wpg = w_mix_pg.rearrange("(kc p) n -> p kc n", p=128)
        for (t, a, b, g) in tiles:
            HK = [("hT", c_, g) for c_ in range(8)]
            nseq_, Ts_ = (2, 256) if t == 0 else (1, 512)
            Lp_ = Ts_ + 16
            pAv = lambda gi_: pinA[:, gi_, 0:nseq_ * Lp_].rearrange("p (s l) -> p s l", l=Lp_)
            pAall = pinA[:, :, 0:nseq_ * Lp_].rearrange("p c (s l) -> p c s l", l=Lp_)
            P.op("pool", lambda e: e.memset(pinA[:, :, :], 0.0), reads=[], writes=["pinA"])
            if t > 0:
                if t == 1:
                    pick_halo(hal[:, :, 0:8], 0, V("ohL"))
                    v_copy(hal[:, :, 8:16], hT[:, :, b:b + 8], [("hT", c_, 1) for c_ in range(8)], ["hal"], eng="pool")
                else:
                    v_copy(hal[:, :, 0:8], hT[:, :, a - 8:a], [("hT", c_, 1) for c_ in range(8)], ["hal"], eng="pool")
                    pick_halo(hal[:, :, 8:16], 8, V("ohR"))
            for ti in range(10):
                w_ = wg[ti % 2]
                dma("pool", w_, wpg[:, :, ti * 256:(ti + 1) * 256], [], [("wg", ti % 2)])
                for j2 in range(2):
                    ch = ti * 2 + j2
                    bk = 4 + ch % 2
                    for kc in range(8):
                        mm(banks[bk][:, :], w_[:, kc, j2 * 128:(j2 + 1) * 128], hT[:, kc, a:b], kc == 0, kc == 7,
                           [("wg", ti % 2)] + HK, BK(bk))
                    if ch < 4:
                        ev(pAv(ch)[:, :, 8:8 + Ts_], banks[bk][:, :].rearrange("p (s l) -> p s l", l=Ts_), BK(bk), ["pinA"])
                        if t > 0:
                            for kc in range(8):
                                mm(banks[6][:, 0:16], w_[:, kc, j2 * 128:(j2 + 1) * 128], hal[:, kc, :], kc == 0, kc == 7,
                                   [("wg", ti % 2), "hal"], BK(6))
                            a_copy(pinA[:, ch, 0:8], banks[6][:, 0:8], BK(6), ["pinA"])
                            a_copy(pinA[:, ch, 520:528], banks[6][:, 8:16], BK(6), ["pinA"])
                    else:
                        a_act(gates[:, ch - 4, :], banks[bk][:, :], AF.Sigmoid, BK(bk), [("gates", ch - 4)])
            nseq, Ts = (2, 256) if t == 0 else (1, 512)
            Lp = Ts + 16
            if t == 0:
                sclL, sclR = C("psclL_p"), C("psclR_p")
            else:
                sclL, sclR = V("psclL_s%d" % t), V("psclR_s%d" % t)
            for gi in range(4):
                w = POOL_W[gi]
                pv = lambda buf: buf[:, 0:nseq * Lp].rearrange("p (s l) -> p s l", l=Lp)
                src = pAv(gi)
                ks = "pinA"
                work = [(pv(pW[0]), "pw0"), (pv(pW[1]), "pw1")]
                sh = 1
                cur = Lp
                for lvl in range(gi + 1):
                    cur -= sh
                    dst, kd = work[lvl % 2]
                    v_tt(dst[:, :, 0:cur], src[:, :, 0:cur], src[:, :, sh:sh + cur], ALU.add, [ks], [kd],
                         eng=("pool" if lvl % 2 else "dve"))
                    src, ks = dst, kd
                    sh *= 2
                res, kres = work[(gi + 1) % 2]
                off = 8 - w // 2
                v_ts(res[:, :, 8:8 + Ts], src[:, :, off:off + Ts], 1.0 / w, None, ALU.mult, None, [ks], [kres])
                v_tt(res[:, :, 8:16], src[:, :, off:off + 8],
                     sclL[:, gi * 8:gi * 8 + 8].unsqueeze(1).to_broadcast([128, nseq, 8]), ALU.mult,
                     [ks, kres, "vecs", "cons"], [kres], eng="pool")
                v_tt(res[:, :, Ts:Ts + 8], src[:, :, off + Ts - 8:off + Ts],
                     sclR[:, gi * 8:gi * 8 + 8].unsqueeze(1).to_broadcast([128, nseq, 8]), ALU.mult,
                     [ks, kres, "vecs", "cons"], [kres], eng="pool")
                v_tt(ub[:, gi, :].rearrange("p (s l) -> p s l", l=Ts), res[:, :, 8:8 + Ts], pAv(gi)[:, :, 8:8 + Ts],
                     ALU.subtract, [kres, "pinA"], [("ubin", gi)])
            for gi in range(4):
                bk = 4 + gi % 2
                mm(banks[bk][:, :], wpl[:, gi, :], ub[:, gi, :], True, True, ["wpl", ("ubin", gi)], BK(bk))
                a_copy(ub[:, gi, :], banks[bk][:, :], BK(bk), [("ub", gi)], scale=V("pool_scale", gi, gi + 1))
            yf = yfin_p if t == 0 else yfin_s[:, :, (t - 1) * 512:t * 512]
            if before_ya is not None:
                before_ya()
                before_ya = None
            dma("pool", worw, worw_d.rearrange("(c p) n -> p c n", p=128), [], ["worw"])
            for oc in range(8):
                for pr in range(8):
                    mm(banks[4][:, :], worw[:, pr, oc * 128:(oc + 1) * 128], yf[:, pr, :], pr == 0, pr == 7,
                       ["worw", ("yfin", pr, t)], BK(4))
                for gi in range(4):
                    mm(banks[5][:, :], wopl[:, gi, oc * 128:(oc + 1) * 128], ub[:, gi, :], gi == 0, gi == 3,
                       ["wopl", ("ub", gi)], BK(5))
                tm = TP[oc % 2]
                v_tt(tm, gates[:, oc, :], banks[4][:, :], ALU.mult, [("gates", oc)] + BK(4), [TPK[oc % 2]])
                tm2 = TP[2 + oc % 2]
                v_tt(tm2, gates[:, 8 + oc, :], banks[5][:, :], ALU.mult, [("gates", 8 + oc)] + BK(5), [TPK[2 + oc % 2]])
                v_tt(merged[:, oc, :], tm, tm2, ALU.add, [TPK[oc % 2], TPK[2 + oc % 2]], [("merged", oc)], eng="pool")
            dma("pool", wout, wout_d.rearrange("(c p) n -> p c n", p=128), [], ["worw"])
            for oc2 in range(8):
                bk = 6 + oc2 % 2
                for oc in range(8):
                    mm(banks[bk][:, :], wout[:, oc, oc2 * 128:(oc2 + 1) * 128], merged[:, oc, :], oc == 0, oc == 7,
                       ["worw", ("merged", oc)], BK(bk))
                v_stt(xT[:, oc2, a:b], banks[bk][:, :], mod[:, 5, oc2, g:g + 1], xT[:, oc2, a:b], ALU.mult, ALU.add,
                      BK(bk) + [("xT", oc2, t)], [("xT", oc2, t)])

    if stage == 2:
        merge_phase([TT[0]])
        P.barrier()
        layer_norm(ln_final(1))
        P.muted = False
        P.barrier()
        write_out(y_out, 0, NT)
        return finish()

    merge_phase([TT[0]])
    sample_mixer()
    for pl in range(2):
        P.op("pool", lambda e, pl=pl: e.collective_compute("AllGather", ALU.bypass, replica_groups=RG,
                                                           ins=[ag2_in[pl].opt()], outs=[ag2_out[pl].opt()]),
             reads=[("ag2in", pl)], writes=[("ag2", pl)], kind="cc")

    def load_yfin(e, pl):
        r = e.alloc_register("qown%d" % pl)
        e.reg_load(r, qidx_d[0:1, 2:3])
        qv = e.snap(r, min_val=0, max_val=3)
        src = ag2_out[pl].rearrange("(r p) t -> p r t", p=128)[:, :, bass.ds(qv * 1024, 1024)]
        dst = yfin_s.rearrange("p (r two) n -> p r two n", two=2)[:, :, pl, :]
        return e.dma_start(out=dst, in_=src)

    def fetch_yfin():
        for pl in range(2):
            P.op("pool", lambda e, pl=pl: load_yfin(e, pl), reads=[("ag2", pl)],
                 writes=[("yfin", pr_, t_) for pr_ in range(8) for t_ in (1, 2)], kind="d")

    merge_phase(TT[1:], before_ya=fetch_yfin)
    P.barrier()
    if stage == 3:
        layer_norm(ln_final(1))
        P.muted = False
        P.barrier()
        write_out(y_out, 0, NT)
        return finish()
    layer_norm(ln_outs(1, 5, 6))
    P.barrier()
    ffn(1, 7)
    layer_norm(ln_final(2))
    P.muted = False
    P.barrier()
    write_out(y_out, 0, NT)
    return finish()


_CACHE = {}


def _host_inputs(inputs):
    g = lambda k: np.asarray(inputs[k][0], np.float32)
    w_mix = g("w_mix_in")
    common = {
        "w_mod": g("w_mod"), "ffn_in": g("ffn_in"), "ffn_out": g("ffn_out"),
        "w_mix_pp": np.ascontiguousarray(w_mix[:, _mix_cols(range(8))]),
        "w_mix_pg": np.ascontiguousarray(w_mix[:, 3456:]),
        "wup": np.ascontiguousarray(g("w_up").reshape(128, D)),
        "aup": np.ascontiguousarray(g("a_up").reshape(128, D)),
        "gup": g("g_up"),
        "w_o_rwkv": g("w_o_rwkv"), "w_pool": np.ascontiguousarray(g("w_pool").reshape(512, 128)),
        "w_o_pool": g("w_o_pool"), "w_out": g("w_out"),
    }
    cp = _const_pack()
    cons = cp.build()
    in_maps = []
    vp0 = None
    for core in range(8):
        b, q = core // 4, core % 4
        vp = _vec_layout(inputs, core)
        if vp0 is None:
            vp0 = vp
        xin = np.concatenate([np.asarray(inputs["x_prompt"])[2 * core:2 * core + 2].reshape(NP_, D),
                              np.asarray(inputs["x_sample"])[b, q * NS_:(q + 1) * NS_]], axis=0)
        cvec = np.stack([_fm(inputs["c_ctx"]), _fm(np.asarray(inputs["c"])[b])], axis=2).reshape(128, 16)
        st = np.asarray(inputs["state_rwkv"], np.float32)[b, 0]
        s0T = np.ascontiguousarray(st[:, 4 * q:4 * q + 4].transpose(0, 1, 3, 2)).reshape(512, 64)
        m = {
            "xin": np.ascontiguousarray(xin, np.float32),
            "cvec": np.ascontiguousarray(cvec, np.float32),
            "vecs": vp.build(), "cons": cons, "cmask": cp.masks,
            "w_mix_ss": np.ascontiguousarray(w_mix[:, _mix_cols([2 * q, 2 * q + 1])]),
            "wup_s": np.ascontiguousarray(common["wup"][:, 256 * q:256 * q + 256]),
            "aup_s": np.ascontiguousarray(common["aup"][:, 256 * q:256 * q + 256]),
            "gup_s": np.ascontiguousarray(common["gup"][:, 256 * q:256 * q + 256]),
            "s0T": s0T,
            "qidx": np.array([[max(q - 1, 0), min(q + 1, 3), q, 0]], np.int32),
        }
        m.update(common)
        in_maps.append(m)
    return in_maps, vp0, cp


def kernel(**inputs):
    in_maps, vp0, cp = _host_inputs(inputs)
    if "prog" not in _CACHE:
        _CACHE["prog"] = build_program(vp0.cols, cp.cols, vp0.n, cp.n)
    nc = _CACHE["prog"]
    res = run_bass_kernel_spmd(nc, in_maps, core_ids=list(range(8)))
    outs = [r["y_out"] for r in res.results]
    y_p = np.stack([o[:NP_] for o in outs]).reshape(16, 256, D).astype(np.float32)
    y_s = np.stack([o[NP_:] for o in outs]).reshape(2, 4096, D).astype(np.float32)
    st = np.stack([r["st_out"].reshape(2, 2, 16, 64, 64) for r in res.results]).reshape(16, 1, 2, 16, 64, 64)
    return y_p, y_s, st.astype(np.float32)
```
